# Optimizing a Trainium2 kernel written in Bass

```python
import math
import jax, jax.numpy as jnp
from jax import lax
import numpy as np

D_MODEL = 2048
BATCH = 8
SEQ = 2048
DEPTH = 2

HEAD_DIM = 64
N_Q_HEADS = D_MODEL // (2 * HEAD_DIM)
N_KV_HEADS = N_Q_HEADS // 4
WINDOW = 128
ROPE_THETA = 10000.0
SSM_HEADS = D_MODEL // (2 * HEAD_DIM)
SSM_HEAD_DIM = 64
D_INNER = SSM_HEADS * SSM_HEAD_DIM
D_STATE = 128
N_GROUPS = 2
CONV_WIDTH = 4
CHUNK = 128
RWKV_HEADS = D_MODEL // HEAD_DIM
DECAY_LORA = 96
AAA_LORA = 96
GATE_LORA = 256
N_SHIFT_MIX = 6
D_FF = 4 * D_MODEL
NORM_EPS = 1e-6
GN_EPS = 64e-5

Q_W = N_Q_HEADS * HEAD_DIM
KV_W = N_KV_HEADS * HEAD_DIM
BC_W = N_GROUPS * D_STATE
CONV_DIM = D_INNER + 2 * BC_W
IN_W = Q_W + 2 * KV_W + D_INNER + CONV_DIM + SSM_HEADS
MIX_W = Q_W + D_INNER
N_EVEN = (DEPTH + 1) // 2
N_ODD = DEPTH // 2

kernel_name = "hybrid_swa_ssd_rwkv7_trunk"


def rms_norm(x, g, eps=NORM_EPS):
    xf = x.astype(jnp.float32)
    y = xf * lax.rsqrt(jnp.mean(xf * xf, axis=-1, keepdims=True) + eps)
    return (y * g.astype(jnp.float32)).astype(x.dtype)


def apply_rope(x, cos, sin):
    x1, x2 = jnp.split(x, 2, axis=-1)
    c = cos[None, :, None, :].astype(x.dtype)
    s = sin[None, :, None, :].astype(x.dtype)
    return jnp.concatenate([x1 * c - x2 * s, x2 * c + x1 * s], axis=-1)


def sliding_window_attention(q, k, v, sinks):
    b_, L = q.shape[0], q.shape[1]
    nb = L // WINDOW
    rep = N_Q_HEADS // N_KV_HEADS
    qb = q.reshape(b_, nb, WINDOW, N_KV_HEADS, rep, HEAD_DIM)
    kb = k.reshape(b_, nb, WINDOW, N_KV_HEADS, HEAD_DIM)
    vb = v.reshape(b_, nb, WINDOW, N_KV_HEADS, HEAD_DIM)

    def with_prev(t):
        prev = jnp.concatenate([jnp.zeros_like(t[:, :1]), t[:, :-1]], axis=1)
        return jnp.concatenate([prev, t], axis=2)

    kw, vw = with_prev(kb), with_prev(vb)
    s = jnp.einsum('bnqhrd,bnkhd->bnhrqk', qb, kw).astype(jnp.float32) * (HEAD_DIM ** -0.5)
    qi = jnp.arange(WINDOW)[:, None] + WINDOW
    kj = jnp.arange(2 * WINDOW)[None, :]
    band = (kj <= qi) & (qi - kj < WINDOW)
    first = band & (kj >= WINDOW)
    mask = jnp.where((jnp.arange(nb) == 0)[:, None, None], first[None], band[None])
    s = jnp.where(mask[None, :, None, None], s, -jnp.inf)
    sink = sinks.astype(jnp.float32).reshape(N_KV_HEADS, rep)[None, None, :, :, None, None]
    m = jnp.maximum(jnp.max(s, axis=-1, keepdims=True), sink)
    p = jnp.exp(s - m)
    p = p / (jnp.sum(p, axis=-1, keepdims=True) + jnp.exp(sink - m))
    o = jnp.einsum('bnhrqk,bnkhd->bnqhrd', p.astype(v.dtype), vw)
    return o.reshape(b_, L, Q_W)


def causal_depthwise_conv(x, w, b):
    y = lax.conv_general_dilated(
        x, w[:, None, :].astype(x.dtype), window_strides=(1,),
        padding=[(CONV_WIDTH - 1, 0)], dimension_numbers=('NWC', 'WIO', 'NWC'),
        feature_group_count=x.shape[-1])
    return y + b.astype(x.dtype)


def ssd_chunked(xs, dt, A, Bm, Cm):
    b_, L = xs.shape[0], xs.shape[1]
    nc = L // CHUNK
    hg = SSM_HEADS // N_GROUPS
    x = (xs.astype(jnp.float32) * dt[..., None]).reshape(b_, nc, CHUNK, N_GROUPS, hg, SSM_HEAD_DIM)
    Bc = Bm.astype(jnp.float32).reshape(b_, nc, CHUNK, N_GROUPS, D_STATE)
    Cc = Cm.astype(jnp.float32).reshape(b_, nc, CHUNK, N_GROUPS, D_STATE)
    cs = jnp.cumsum((dt * A).reshape(b_, nc, CHUNK, N_GROUPS, hg), axis=2)
    causal = jnp.tril(jnp.ones((CHUNK, CHUNK), dtype=bool))[None, None, :, :, None, None]
    seg = cs[:, :, :, None] - cs[:, :, None]
    Lm = jnp.exp(jnp.where(causal, seg, -jnp.inf))
    CB = jnp.einsum('bcqgn,bckgn->bcqkg', Cc, Bc)
    y_diag = jnp.einsum('bcqkg,bcqkgh,bckghp->bcqghp', CB, Lm, x)
    decay_out = jnp.exp(cs[:, :, -1:] - cs)
    states = jnp.einsum('bckgn,bckgh,bckghp->bcghpn', Bc, decay_out, x)
    chunk_decay = jnp.exp(cs[:, :, -1])

    def step(h, inp):
        s_c, d_c = inp
        return h * d_c[..., None, None] + s_c, h

    h0 = jnp.zeros((b_, N_GROUPS, hg, SSM_HEAD_DIM, D_STATE), jnp.float32)
    _, prev = lax.scan(step, h0, (jnp.moveaxis(states, 1, 0), jnp.moveaxis(chunk_decay, 1, 0)))
    prev = jnp.moveaxis(prev, 0, 1)
    y_off = jnp.einsum('bcqgn,bcghpn,bcqgh->bcqghp', Cc, prev, jnp.exp(cs))
    return (y_diag + y_off).reshape(b_, L, SSM_HEADS, SSM_HEAD_DIM)


def attn_ssd_mixer(h, cos, sin, w_in, conv_w, conv_b, dt_bias, a_log, d_skip, ssd_norm,
                   q_norm, k_norm, sinks, w_out):
    b_, L, _ = h.shape
    proj = h @ w_in.astype(h.dtype)
    splits = [Q_W, Q_W + KV_W, Q_W + 2 * KV_W, Q_W + 2 * KV_W + D_INNER,
              Q_W + 2 * KV_W + D_INNER + CONV_DIM]
    q, k, v, z, xbc, dt_raw = jnp.split(proj, splits, axis=-1)
    q = apply_rope(rms_norm(q.reshape(b_, L, N_Q_HEADS, HEAD_DIM), q_norm), cos, sin)
    k = apply_rope(rms_norm(k.reshape(b_, L, N_KV_HEADS, HEAD_DIM), k_norm), cos, sin)
    v = v.reshape(b_, L, N_KV_HEADS, HEAD_DIM)
    y_attn = sliding_window_attention(q, k, v, sinks)
    xbc = jax.nn.silu(causal_depthwise_conv(xbc, conv_w, conv_b))
    xs, Bm, Cm = jnp.split(xbc, [D_INNER, D_INNER + BC_W], axis=-1)
    xs = xs.reshape(b_, L, SSM_HEADS, SSM_HEAD_DIM)
    Bm = Bm.reshape(b_, L, N_GROUPS, D_STATE)
    Cm = Cm.reshape(b_, L, N_GROUPS, D_STATE)
    dt = jax.nn.softplus(dt_raw.astype(jnp.float32) + dt_bias.astype(jnp.float32))
    A = -jnp.exp(a_log.astype(jnp.float32))
    y = ssd_chunked(xs, dt, A, Bm, Cm) + d_skip.astype(jnp.float32)[:, None] * xs.astype(jnp.float32)
    y = y.reshape(b_, L, D_INNER) * jax.nn.silu(z.astype(jnp.float32))
    yg = y.reshape(b_, L, N_GROUPS, D_INNER // N_GROUPS)
    yg = yg * lax.rsqrt(jnp.mean(yg * yg, axis=-1, keepdims=True) + NORM_EPS)
    y_ssd = (yg.reshape(b_, L, D_INNER) * ssd_norm.astype(jnp.float32)).astype(h.dtype)
    return jnp.concatenate([y_attn.astype(h.dtype), y_ssd], axis=-1) @ w_out.astype(h.dtype)


def rwkv7_scan(r, w, k, v, a, b):
    def step(S, inp):
        r_t, w_t, k_t, v_t, a_t, b_t = inp
        sa = jnp.einsum('bhvk,bhk->bhv', S, a_t)
        S = S * w_t[:, :, None, :] + sa[..., None] * b_t[:, :, None, :] + v_t[..., None] * k_t[:, :, None, :]
        return S, jnp.einsum('bhvk,bhk->bhv', S, r_t)

    seq = tuple(jnp.moveaxis(t.astype(jnp.float32), 1, 0) for t in (r, w, k, v, a, b))
    b_ = r.shape[0]
    S0 = jnp.zeros((b_, RWKV_HEADS, HEAD_DIM, HEAD_DIM), jnp.float32)
    _, y = lax.scan(step, S0, seq)
    return jnp.moveaxis(y, 0, 1)


def rwkv7_time_mix(h, mix, w0, w1, w2, a0, a1, a2, g1, g2, k_k, k_a, r_k,
                   w_r, w_k, w_v, w_o, ln_w, ln_b):
    b_, L, D = h.shape
    dt_ = h.dtype
    xx = jnp.pad(h, ((0, 0), (1, 0), (0, 0)))[:, :-1] - h
    mix = mix.astype(dt_)
    xr, xw, xk, xv, xa, xg = (h + xx * mix[i] for i in range(N_SHIFT_MIX))
    r = xr @ w_r.astype(dt_)
    k = xk @ w_k.astype(dt_)
    v = xv @ w_v.astype(dt_)
    logw = -jax.nn.softplus(-(w0.astype(jnp.float32)
                              + (jnp.tanh(xw @ w1.astype(dt_)) @ w2.astype(dt_)).astype(jnp.float32))) - 0.5
    decay = jnp.exp(-jnp.exp(logw))
    a = jax.nn.sigmoid((a0.astype(dt_) + (xa @ a1.astype(dt_)) @ a2.astype(dt_)).astype(jnp.float32))
    g = jax.nn.sigmoid(xg @ g1.astype(dt_)) @ g2.astype(dt_)
    kf = k.astype(jnp.float32)
    kk = (kf * k_k.astype(jnp.float32)).reshape(b_, L, RWKV_HEADS, HEAD_DIM)
    kk = kk * lax.rsqrt(jnp.maximum(jnp.sum(kk * kk, axis=-1, keepdims=True), 1e-24))
    kf = kf * (1.0 + (a - 1.0) * k_a.astype(jnp.float32))
    hs = lambda t: t.reshape(b_, L, RWKV_HEADS, HEAD_DIM)
    rh, kh, vh, ah = hs(r.astype(jnp.float32)), hs(kf), hs(v.astype(jnp.float32)), hs(a)
    y = rwkv7_scan(rh, hs(decay), kh, vh, -kk, kk * ah)
    mu = jnp.mean(y, axis=-1, keepdims=True)
    var = jnp.mean(jnp.square(y - mu), axis=-1, keepdims=True)
    y = ((y - mu) * lax.rsqrt(var + GN_EPS)).reshape(b_, L, D)
    y = hs(y * ln_w.astype(jnp.float32) + ln_b.astype(jnp.float32))
    bonus = jnp.sum(rh * kh * r_k.astype(jnp.float32), axis=-1, keepdims=True) * vh
    y = (y + bonus).reshape(b_, L, D).astype(dt_)
    return (y * g) @ w_o.astype(dt_)


def squared_relu_mlp(h, w_up, w_down):
    u = h @ w_up.astype(h.dtype)
    return jnp.square(jax.nn.relu(u)) @ w_down.astype(h.dtype)


def setup_inputs(seed: int = 0) -> dict:
    key = jax.random.key(seed)
    ks = iter(jax.random.split(key, 48))

    def nrm(shape, scale):
        return scale * jax.random.normal(next(ks), shape, jnp.float32)

    def uni(shape, lo, hi):
        return jax.random.uniform(next(ks), shape, jnp.float32, lo, hi)

    D = D_MODEL
    x = nrm((BATCH, SEQ, D), 1.0)
    norm_mix = 1.0 + nrm((DEPTH, D), 0.02)
    norm_ffn = 1.0 + nrm((DEPTH, D), 0.02)
    w_up = nrm((DEPTH, D, D_FF), D ** -0.5)
    w_down = nrm((DEPTH, D_FF, D), D_FF ** -0.5)
    ev_w_in = nrm((N_EVEN, D, IN_W), D ** -0.5)
    ev_conv_w = nrm((N_EVEN, CONV_WIDTH, CONV_DIM), CONV_WIDTH ** -0.5)
    ev_conv_b = nrm((N_EVEN, CONV_DIM), 0.02)
    dt0 = jnp.exp(uni((N_EVEN, SSM_HEADS), math.log(1e-3), math.log(1e-1)))
    ev_dt_bias = dt0 + jnp.log(-jnp.expm1(-dt0))
    ev_a_log = jnp.log(uni((N_EVEN, SSM_HEADS), 1.0, 16.0))
    ev_d_skip = 1.0 + nrm((N_EVEN, SSM_HEADS), 0.1)
    ev_ssd_norm = 1.0 + nrm((N_EVEN, D_INNER), 0.02)
    ev_q_norm = 1.0 + nrm((N_EVEN, HEAD_DIM), 0.02)
    ev_k_norm = 1.0 + nrm((N_EVEN, HEAD_DIM), 0.02)
    ev_sinks = nrm((N_EVEN, N_Q_HEADS), 0.5)
    ev_w_out = nrm((N_EVEN, MIX_W, D), MIX_W ** -0.5)
    od_mix = uni((N_ODD, N_SHIFT_MIX, D), 0.0, 1.0)
    od_w0 = uni((N_ODD, D), -6.0, -1.0)
    od_w1 = nrm((N_ODD, D, DECAY_LORA), D ** -0.5)
    od_w2 = nrm((N_ODD, DECAY_LORA, D), 0.1 * DECAY_LORA ** -0.5)
    od_a0 = nrm((N_ODD, D), 0.5)
    od_a1 = nrm((N_ODD, D, AAA_LORA), D ** -0.5)
    od_a2 = nrm((N_ODD, AAA_LORA, D), AAA_LORA ** -0.5)
    od_g1 = nrm((N_ODD, D, GATE_LORA), D ** -0.5)
    od_g2 = nrm((N_ODD, GATE_LORA, D), GATE_LORA ** -0.5)
    od_k_k = 0.85 + nrm((N_ODD, D), 0.05)
    od_k_a = 1.0 + nrm((N_ODD, D), 0.05)
    od_r_k = nrm((N_ODD, RWKV_HEADS, HEAD_DIM), 0.1)
    od_w_r = nrm((N_ODD, D, D), D ** -0.5)
    od_w_k = nrm((N_ODD, D, D), D ** -0.5)
    od_w_v = nrm((N_ODD, D, D), D ** -0.5)
    od_w_o = nrm((N_ODD, D, D), D ** -0.5)
    od_ln_w = 1.0 + nrm((N_ODD, D), 0.02)
    od_ln_b = nrm((N_ODD, D), 0.02)
    return {
        "x": x, "norm_mix": norm_mix, "norm_ffn": norm_ffn, "w_up": w_up, "w_down": w_down,
        "ev_w_in": ev_w_in, "ev_conv_w": ev_conv_w, "ev_conv_b": ev_conv_b,
        "ev_dt_bias": ev_dt_bias, "ev_a_log": ev_a_log, "ev_d_skip": ev_d_skip,
        "ev_ssd_norm": ev_ssd_norm, "ev_q_norm": ev_q_norm, "ev_k_norm": ev_k_norm,
        "ev_sinks": ev_sinks, "ev_w_out": ev_w_out,
        "od_mix": od_mix, "od_w0": od_w0, "od_w1": od_w1, "od_w2": od_w2,
        "od_a0": od_a0, "od_a1": od_a1, "od_a2": od_a2, "od_g1": od_g1, "od_g2": od_g2,
        "od_k_k": od_k_k, "od_k_a": od_k_a, "od_r_k": od_r_k,
        "od_w_r": od_w_r, "od_w_k": od_w_k, "od_w_v": od_w_v, "od_w_o": od_w_o,
        "od_ln_w": od_ln_w, "od_ln_b": od_ln_b,
    }


def reference(x, norm_mix, norm_ffn, w_up, w_down,
              ev_w_in, ev_conv_w, ev_conv_b, ev_dt_bias, ev_a_log, ev_d_skip,
              ev_ssd_norm, ev_q_norm, ev_k_norm, ev_sinks, ev_w_out,
              od_mix, od_w0, od_w1, od_w2, od_a0, od_a1, od_a2, od_g1, od_g2,
              od_k_k, od_k_a, od_r_k, od_w_r, od_w_k, od_w_v, od_w_o, od_ln_w, od_ln_b):
    L = x.shape[1]
    inv_freq = ROPE_THETA ** (-jnp.arange(0, HEAD_DIM, 2, dtype=jnp.float32) / HEAD_DIM)
    ang = jnp.arange(L, dtype=jnp.float32)[:, None] * inv_freq[None, :]
    cos, sin = jnp.cos(ang), jnp.sin(ang)
    for i in range(DEPTH):
        j = i // 2
        h = rms_norm(x, norm_mix[i])
        if i % 2 == 0:
            mix = attn_ssd_mixer(h, cos, sin, ev_w_in[j], ev_conv_w[j], ev_conv_b[j],
                                 ev_dt_bias[j], ev_a_log[j], ev_d_skip[j], ev_ssd_norm[j],
                                 ev_q_norm[j], ev_k_norm[j], ev_sinks[j], ev_w_out[j])
        else:
            mix = rwkv7_time_mix(h, od_mix[j], od_w0[j], od_w1[j], od_w2[j], od_a0[j],
                                 od_a1[j], od_a2[j], od_g1[j], od_g2[j], od_k_k[j], od_k_a[j],
                                 od_r_k[j], od_w_r[j], od_w_k[j], od_w_v[j], od_w_o[j],
                                 od_ln_w[j], od_ln_b[j])
        x = x + mix.astype(x.dtype)
        h = rms_norm(x, norm_ffn[i])
        x = x + squared_relu_mlp(h, w_up[i], w_down[i]).astype(x.dtype)
    return x
```

```python
from contextlib import ExitStack
import math
import numpy as np
import concourse.bass as bass
import concourse.mybir as mybir
from concourse.bass_utils import run_bass_kernel_spmd

F32 = mybir.dt.float32
BF16 = mybir.dt.bfloat16
I32 = mybir.dt.int32
AF = mybir.ActivationFunctionType
ALU = mybir.AluOpType
AX = mybir.AxisListType

L = 2048
D = 2048
DFF = 8192
NCORES = 8


class Tok:
    __slots__ = ("w", "r", "name", "dkey")
    registry = []

    def __init__(self, name=""):
        self.w = {}
        self.r = {}
        self.name = name
        self.dkey = None
        Tok.registry.append(self)


class Builder:
    def __init__(self):
        self.nc = bass.Bass("TRN2", target_bir_lowering=False)
        nc = self.nc
        self.E = {"pe": nc.tensor, "dve": nc.vector, "act": nc.scalar, "pool": nc.gpsimd, "sp": nc.sync}
        self.semobj = {}
        self.semval = {}
        self.known = {e: {} for e in self.E}
        for e in self.E:
            k = "e:" + e
            self.semobj[k] = nc.alloc_semaphore(name="es_" + e)
            self.semval[k] = 0
        self.ndma = 0
        self.free_dkeys = []
        self.ninstr = 0
        self.rr = 0

    def _dkey(self, tok):
        if tok.dkey is None:
            if self.free_dkeys:
                tok.dkey = self.free_dkeys.pop()
            else:
                k = "d:%d" % self.ndma
                self.ndma += 1
                self.semobj[k] = self.nc.alloc_semaphore(name="ds_%d" % (self.ndma - 1))
                self.semval[k] = 0
                tok.dkey = k
        return tok.dkey

    def release(self, toks):
        for t in toks:
            if t.dkey is not None:
                self.free_dkeys.append(t.dkey)
                t.dkey = None

    def _wait(self, x, key, val):
        if self.known[x].get(key, 0) >= val:
            return
        self.E[x].wait_ge(self.semobj[key], val)
        self.known[x][key] = val
        self.ninstr += 1

    def _deps(self, x, reads, writes, pwrites):
        own = "e:" + x
        need = {}

        def add(k, v, raw):
            if k == own and (not raw or x == "pe"):
                return
            if need.get(k, 0) < v:
                need[k] = v

        for t in reads:
            for k, v in t.w.items():
                add(k, v, True)
        for t in list(writes) + list(pwrites):
            for k, v in t.r.items():
                add(k, v, False)
        for t in writes:
            for k, v in t.w.items():
                add(k, v, False)
        for k, v in need.items():
            self._wait(x, k, v)

    def _record(self, key, val, reads, writes, pwrites):
        for t in reads:
            if t.r.get(key, 0) < val:
                t.r[key] = val
        for t in writes:
            t.w = {key: val}
        for t in pwrites:
            if t.w.get(key, 0) < val:
                t.w[key] = val

    def op(self, x, fn, r=(), w=(), pw=(), inc=True):
        self._deps(x, r, w, pw)
        ins = fn(self.E[x])
        key = "e:" + x
        if inc:
            ins.then_inc(self.semobj[key], 1)
            self.semval[key] += 1
            val = self.semval[key]
        else:
            val = self.semval[key] + 1
        self._record(key, val, r, w, pw)
        self.ninstr += 1
        return ins

    def V(self, fn, r=(), w=(), pw=()):
        return self.op("dve", fn, r, w, pw)

    def A(self, fn, r=(), w=(), pw=()):
        return self.op("act", fn, r, w, pw)

    def G(self, fn, r=(), w=(), pw=()):
        return self.op("pool", fn, r, w, pw)

    def T(self, fn, r=(), w=(), pw=(), inc=True):
        return self.op("pe", fn, r, w, pw, inc)

    def VG(self, fn, r=(), w=(), pw=()):
        self.rr += 1
        return self.op("dve" if self.rr % 2 else "pool", fn, r, w, pw)

    def VA(self, fn_v, fn_a, r=(), w=(), pw=()):
        self.rr += 1
        if self.rr % 2:
            return self.op("dve", fn_v, r, w, pw)
        return self.op("act", fn_a, r, w, pw)

    def dma(self, q, out, in_, r=(), w=(), pw=(), semtok=None, **kw):
        self._deps(q, r, w, pw)
        if semtok is None:
            semtok = (list(w) + list(pw) + list(r))[0]
        key = self._dkey(semtok)
        ins = self.E[q].dma_start(out=out, in_=in_, **kw)
        ins.then_inc(self.semobj[key], 16)
        self.semval[key] += 16
        self._record(key, self.semval[key], r, w, pw)
        self.ninstr += 1
        return ins

    def barrier(self):
        for x in self.E:
            for k, v in self.semval.items():
                if k == "e:" + x or v == 0:
                    continue
                self._wait(x, k, v)
        for t in Tok.registry:
            t.w = {}
            t.r = {}
            if t.dkey is not None:
                self.free_dkeys.append(t.dkey)
                t.dkey = None

    def finish(self):
        for k, v in self.semval.items():
            if k.startswith("d:") and v > 0:
                self._wait("sp", k, v)


class Pool_:
    def __init__(self, es, nc, name, shape, dtype, n):
        self.tiles = [es.enter_context(nc.sbuf_tensor("%s%d" % (name, i), shape, dtype)) for i in range(n)]
        self.toks = [Tok("%s%d" % (name, i)) for i in range(n)]
        self.i = 0

    def next(self):
        j = self.i % len(self.tiles)
        self.i += 1
        return self.tiles[j], self.toks[j]


class Ctx:
    pass


def host_consts():
    c = {}
    ident = np.eye(128, dtype=np.float32)
    c["c_ident"] = ident
    c["c_ones"] = np.ones((128, 128), np.float32)
    blk = np.zeros((128, 128), np.float32)
    blk[:64, :64] = 1
    blk[64:, 64:] = 1
    c["c_blk"] = blk
    k = np.arange(128)[:, None]
    q = np.arange(128)[None, :]
    c["c_mcur"] = (q >= k).astype(np.float32)
    c["c_mprev"] = (k > q).astype(np.float32)
    c["c_utri"] = (k <= q).astype(np.float32)
    c["c_sutri"] = (k < q).astype(np.float32)
    neg = np.where(q >= k, 0.0, -30000.0).astype(np.float32)
    c["c_negm"] = neg
    last = np.zeros((128, 128), np.float32)
    last[127, :] = 1
    c["c_last"] = last
    return c


CONST_NAMES = ["c_ident", "c_ones", "c_blk", "c_mcur", "c_mprev", "c_utri", "c_sutri", "c_negm", "c_last"]

WEIGHT_SPECS = {
    "norm_mix": [2, 2048], "norm_ffn": [2, 2048], "w_up": [2, 2048, 8192], "w_down": [2, 8192, 2048],
    "ev_w_in": [1, 2048, 4112], "ev_conv_w": [1, 4, 1536], "ev_conv_b": [1, 1536], "ev_dt_bias": [1, 16],
    "ev_a_log": [1, 16], "ev_d_skip": [1, 16], "ev_ssd_norm": [1, 1024], "ev_q_norm": [1, 64],
    "ev_k_norm": [1, 64], "ev_sinks": [1, 16], "ev_w_out": [1, 2048, 2048], "od_mix": [1, 6, 2048],
    "od_w0": [1, 2048], "od_w1": [1, 2048, 96], "od_w2": [1, 96, 2048], "od_a0": [1, 2048],
    "od_a1": [1, 2048, 96], "od_a2": [1, 96, 2048], "od_g1": [1, 2048, 256], "od_g2": [1, 256, 2048],
    "od_k_k": [1, 2048], "od_k_a": [1, 2048], "od_r_k": [1, 32, 64], "od_w_r": [1, 2048, 2048],
    "od_w_k": [1, 2048, 2048], "od_w_v": [1, 2048, 2048], "od_w_o": [1, 2048, 2048], "od_ln_w": [1, 2048],
    "od_ln_b": [1, 2048],
}


class Prog:
    def __init__(self, ext_in=(), ext_out=(), nb=1):
        Tok.registry = []
        self.b = Builder()
        self.nc = self.b.nc
        self.nb = nb
        self.ext_in = set(ext_in)
        self.ext_out = set(ext_out)
        self.dram = {}
        self.dtok = {}
        self.inputs_used = []
        self.es = ExitStack()
        nc = self.nc
        self.ps = [self.es.enter_context(nc.psum_tensor("psb%d" % i, [128, 512], F32)) for i in range(8)]
        self.pst = [Tok("psb%d" % i) for i in range(8)]
        self._consts()

    def inp(self, name, shape=None):
        if name not in self.dram:
            shp = shape if shape is not None else WEIGHT_SPECS[name]
            self.dram[name] = self.nc.dram_tensor(name, list(shp), F32, kind="ExternalInput").ap()
            self.dtok[name] = Tok(name)
            self.inputs_used.append(name)
        return self.dram[name]

    def scratch(self, name, shape, dtype):
        if name not in self.dram:
            kind = "Internal"
            if name in self.ext_in:
                kind = "ExternalInput"
                self.inputs_used.append(name)
            elif name in self.ext_out:
                kind = "ExternalOutput"
            self.dram[name] = self.nc.dram_tensor(name, list(shape), dtype, kind=kind).ap()
            self.dtok[name] = Tok(name)
        return self.dram[name], self.dtok[name]

    def dbg(self, name, ap, tok, shape, dtype):
        if name in self.ext_out:
            d, dtk = self.scratch(name, shape, dtype)
            self.b.dma("sp", d, ap, r=[tok], pw=[dtk])

    def sb(self, es, name, shape, dtype):
        return es.enter_context(self.nc.sbuf_tensor(name, list(shape), dtype)), Tok(name)

    def _consts(self):
        b, nc, es = self.b, self.nc, self.es
        self.c = {}
        self.ctok = {}
        for n in CONST_NAMES:
            src = self.inp(n, [128, 128])
            t, tk = self.sb(es, "s_" + n, [128, 128], F32)
            b.dma("sp", t[:], src[:, :], w=[tk])
            self.c[n] = t
            self.ctok[n] = tk
        for n in ["c_ident", "c_ones", "c_mcur", "c_mprev", "c_blk"]:
            t, tk = self.sb(es, "s_" + n + "_b", [128, 128], BF16)
            b.V(lambda e, t=t, n=n: e.tensor_copy(out=t[:], in_=self.c[n][:]), r=[self.ctok[n]], w=[tk])
            self.c[n + "_b"] = t
            self.ctok[n + "_b"] = tk

    def load_vec_fm(self, es, name, src_ap_1d, nchunk, q="sp"):
        t, tk = self.sb(es, name, [128, nchunk], F32)
        self.b.dma(q, t[:], src_ap_1d.rearrange("(c p) -> p c", p=128), w=[tk], allow_slow_non_contiguous=True)
        return t, tk

    def norm_transpose(self, pools, xin, xin_tok, row0, ntok, g_sb, g_tok, hT, hT_tok, col0, banks):
        b = self.b
        xpool, xspool, stat = pools
        ident = self.c["c_ident"]
        for s in range(ntok // 128):
            xt, xtk = xpool.next()
            b.dma("sp", xt[:], xin[row0 + s * 128: row0 + (s + 1) * 128, :], r=[xin_tok], w=[xtk])
            xs, xstk = xspool.next()
            st, sttk = stat.next()
            b.V(lambda e: e.scalar_tensor_tensor(out=xs[:], in0=xt[:], scalar=1.0, in1=xt[:],
                                                 op0=ALU.mult, op1=ALU.mult, accum_out=st[:, 0:1]),
                r=[xtk], w=[xstk, sttk])
            b.A(lambda e: e.activation(out=st[:, 1:2], in_=st[:, 0:1], func=AF.Sqrt, scale=1.0 / D, bias=self.eps_t[:, 0:1]),
                r=[sttk, self.eps_tok], w=[sttk])
            b.V(lambda e: e.reciprocal(out=st[:, 2:3], in_=st[:, 1:2]), r=[sttk], w=[sttk])
            b.A(lambda e: e.activation(out=xs[:], in_=xt[:], func=AF.Copy, scale=st[:, 2:3]),
                r=[xtk, sttk], w=[xstk])
            for q4 in range(4):
                bk = banks[(s * 4 + q4) % len(banks)]
                ps, pstk = self.ps[bk], self.pst[bk]
                for j in range(4):
                    kc = q4 * 4 + j
                    b.T(lambda e: e.transpose(out=ps[:, j * 128:(j + 1) * 128], in_=xs[:, kc * 128:(kc + 1) * 128],
                                              identity=ident[:]),
                        r=[xstk, self.ctok["c_ident"]], w=[pstk] if j == 0 else [], pw=[pstk] if j else [], inc=(j == 3))
                c_lo = col0 + s * 128
                outap = hT[:, q4 * 4:(q4 + 1) * 4, c_lo:c_lo + 128]
                inap = ps[:, :].rearrange("p (j t) -> p j t", j=4)
                gap = g_sb[:, q4 * 4:(q4 + 1) * 4].unsqueeze(2).to_broadcast([128, 4, 128])
                b.V(lambda e: e.tensor_tensor(out=outap, in0=inap, in1=gap, op=ALU.mult),
                    r=[pstk, g_tok], pw=[hT_tok])

    def make_eps(self):
        t, tk = self.sb(self.es, "eps_t", [128, 2], F32)
        self.b.V(lambda e: e.memset(t[:, 0:1], 1e-6), w=[tk])
        self.b.V(lambda e: e.memset(t[:, 1:2], 64e-5), pw=[tk])
        self.eps_t, self.eps_tok = t, tk

    def load_piece(self, wpool, W, wtok, k0, nkc, c0, ncols):
        wt, wtk = wpool.next()
        src = W[k0:k0 + nkc * 128, c0:c0 + ncols].rearrange("(kc p) n -> p kc n", p=128)
        self.b.dma("pool", wt[:, 0:nkc, 0:ncols], src, r=[wtok], w=[wtk])
        return wt, wtk

    def linear_fm(self, wpool, W, wtok, col_blocks, hT, hT_tok, ntok, evac, banks, nkc=16, k0=0, pw_cols=256):
        b = self.b
        bi = 0
        for (c0, ncols, oc0) in col_blocks:
            wt, wtk = self.load_piece(wpool, W, wtok, k0, nkc, c0, ncols)
            for o in range((ncols + 127) // 128):
                m = min(128, ncols - o * 128)
                for tt in range(ntok // 512):
                    bk = banks[bi % len(banks)]
                    bi += 1
                    ps, pstk = self.ps[bk], self.pst[bk]
                    for kc in range(nkc):
                        b.T(lambda e: e.matmul(ps[0:m, :], lhsT=wt[:, kc, o * 128:o * 128 + m],
                                               rhs=hT[:, kc, tt * 512:(tt + 1) * 512],
                                               start=(kc == 0), stop=(kc == nkc - 1)),
                            r=[wtk, hT_tok], w=[pstk] if kc == 0 else [], pw=[pstk] if kc else [],
                            inc=(kc == nkc - 1))
                    evac(oc0 + o, m, tt, ps, pstk)

    def linear_tm(self, wpool, W, wtok, col_blocks, hT, hT_tok, ntok, evac, banks, nkc=16, k0=0, tcol0=0):
        b = self.b
        bi = 0
        for (c0, ncols, gi) in col_blocks:
            wt, wtk = self.load_piece(wpool, W, wtok, k0, nkc, c0, ncols)
            for tt in range(ntok // 128):
                bk = banks[bi % len(banks)]
                bi += 1
                ps, pstk = self.ps[bk], self.pst[bk]
                for kc in range(nkc):
                    b.T(lambda e: e.matmul(ps[:, 0:ncols], lhsT=hT[:, kc, tcol0 + tt * 128:tcol0 + (tt + 1) * 128],
                                           rhs=wt[:, kc, 0:ncols], start=(kc == 0), stop=(kc == nkc - 1)),
                        r=[wtk, hT_tok], w=[pstk] if kc == 0 else [], pw=[pstk] if kc else [],
                        inc=(kc == nkc - 1))
                evac(gi, ncols, tt, ps, pstk)

    def stage_inproj(self, x_ap, x_tok, row0):
        b, nc = self.b, self.nc
        W = self.inp("ev_w_in")[0]
        wtok = self.dtok["ev_w_in"]
        qkT, qkT_tok = self.scratch("qkT", [1280, L], F32)
        xbcT, xbcT_tok = self.scratch("xbcT", [1536, L], F32)
        z_d, z_tok = self.scratch("z_tm", [L, 1024], F32)
        v_d, v_tok = self.scratch("v_tm", [L, 256], BF16)
        dt_d, dt_tok = self.scratch("dtr_tm", [L, 16], F32)
        with ExitStack() as es:
            g_sb, g_tok = self.load_vec_fm(es, "g_nm0", self.inp("norm_mix")[0], 16)
            hT, hT_tok = self.sb(es, "hT_A", [128, 16, L], BF16)
            xpool = Pool_(es, nc, "xA", [128, D], F32, 2)
            xspool = Pool_(es, nc, "xsA", [128, D], F32, 2)
            stat = Pool_(es, nc, "stA", [128, 4], F32, 2)
            self.norm_transpose((xpool, xspool, stat), x_ap, x_tok, row0, L, g_sb, g_tok, hT, hT_tok, 0, [0, 1])
            wpool = Pool_(es, nc, "wA", [128, 16, 512], BF16, 3)
            opool = Pool_(es, nc, "oA", [128, 512], F32, 4)
            obpool = Pool_(es, nc, "obA", [128, 256], BF16, 2)

            def evac_fm(dst, dst_tok):
                def f(oc, m, tt, ps, pstk):
                    ot, otk = opool.next()
                    b.VA(lambda e: e.tensor_copy(out=ot[:], in_=ps[:]),
                         lambda e: e.activation(out=ot[:], in_=ps[:], func=AF.Copy),
                         r=[pstk], w=[otk])
                    b.dma("sp", dst[oc * 128:(oc + 1) * 128, tt * 512:(tt + 1) * 512], ot[:], r=[otk], pw=[dst_tok])
                return f

            blocks = [(c, 256, c // 128) for c in range(0, 1280, 256)]
            self.linear_fm(wpool, W, wtok, blocks, hT, hT_tok, L, evac_fm(qkT, qkT_tok), [2, 3, 4, 5])
            blocks = [(2560 + c, 256, c // 128) for c in range(0, 1536, 256)]
            self.linear_fm(wpool, W, wtok, blocks, hT, hT_tok, L, evac_fm(xbcT, xbcT_tok), [2, 3, 4, 5])

            def evac_z(gi, ncols, tt, ps, pstk):
                ot, otk = opool.next()
                b.VA(lambda e: e.tensor_copy(out=ot[:], in_=ps[:]),
                     lambda e: e.activation(out=ot[:], in_=ps[:], func=AF.Copy),
                     r=[pstk], w=[otk])
                b.dma("sp", z_d[tt * 128:(tt + 1) * 128, gi * 512:(gi + 1) * 512], ot[:], r=[otk], pw=[z_tok])

            self.linear_tm(wpool, W, wtok, [(1536, 512, 0), (2048, 512, 1)], hT, hT_tok, L, evac_z, [2, 3, 4, 5])

            def evac_v(gi, ncols, tt, ps, pstk):
                ot, otk = obpool.next()
                b.V(lambda e: e.tensor_copy(out=ot[:], in_=ps[:, 0:256]), r=[pstk], w=[otk])
                b.dma("sp", v_d[tt * 128:(tt + 1) * 128, :], ot[:], r=[otk], pw=[v_tok])

            self.linear_tm(wpool, W, wtok, [(1280, 256, 0)], hT, hT_tok, L, evac_v, [2, 3, 4, 5])

            def evac_dt(gi, ncols, tt, ps, pstk):
                ot, otk = opool.next()
                b.A(lambda e: e.activation(out=ot[:, 0:16], in_=ps[:, 0:16], func=AF.Copy), r=[pstk], w=[otk])
                b.dma("sp", dt_d[tt * 128:(tt + 1) * 128, :], ot[:, 0:16], r=[otk], pw=[dt_tok])

            self.linear_tm(wpool, W, wtok, [(4096, 16, 0)], hT, hT_tok, L, evac_dt, [2, 3, 4, 5])
            b.barrier()

    def rope_tables(self, es):
        b, nc = self.b, self.nc
        TWO_PI = 2.0 * math.pi
        posi, posi_tk = self.sb(es, "rp_posi", [64, L], I32)
        b.G(lambda e: e.iota(posi[:], pattern=[[1, L]], base=0, channel_multiplier=0), w=[posi_tk])
        pidx, pidx_tk = self.sb(es, "rp_pidx", [64, 1], I32)
        b.G(lambda e: e.iota(pidx[0:32, :], pattern=[[0, 1]], base=0, channel_multiplier=1), w=[pidx_tk])
        b.G(lambda e: e.iota(pidx[32:64, :], pattern=[[0, 1]], base=0, channel_multiplier=1), pw=[pidx_tk])
        pf, pf_tk = self.sb(es, "rp_pf", [64, 2], F32)
        b.V(lambda e: e.tensor_copy(out=pf[:, 0:1], in_=pidx[:]), r=[pidx_tk], w=[pf_tk])
        b.A(lambda e: e.activation(out=pf[:, 1:2], in_=pf[:, 0:1], func=AF.Exp, scale=-math.log(10000.0) / 32.0),
            r=[pf_tk], w=[pf_tk])
        ang, ang_tk = self.sb(es, "rp_ang", [64, L], F32)
        b.V(lambda e: e.tensor_copy(out=ang[:], in_=posi[:]), r=[posi_tk], w=[ang_tk])
        b.V(lambda e: e.tensor_scalar(out=ang[:], in0=ang[:], scalar1=pf[:, 1:2], scalar2=None, op0=ALU.mult),
            r=[ang_tk, pf_tk], w=[ang_tk])
        sgn, sgn_tk = self.sb(es, "rp_sgn", [64, 1], F32)
        b.V(lambda e: e.memset(sgn[0:32, :], -1.0), w=[sgn_tk])
        b.V(lambda e: e.memset(sgn[32:64, :], 1.0), pw=[sgn_tk])
        cos_t, cos_tk = self.sb(es, "rp_cos", [64, L], F32)
        sin_t, sin_tk = self.sb(es, "rp_sin", [64, L], F32)
        tmp, tmp_tk = self.sb(es, "rp_tmp", [64, L], F32)
        ki, ki_tk = posi, posi_tk
        for (dst, dst_tk, shift) in ((sin_t, sin_tk, 0.0), (cos_t, cos_tk, math.pi / 2)):
            b.V(lambda e: e.tensor_scalar(out=tmp[:], in0=ang[:], scalar1=shift, scalar2=1.0 / TWO_PI,
                                          op0=ALU.add, op1=ALU.mult), r=[ang_tk], w=[tmp_tk])
            b.V(lambda e: e.tensor_copy(out=ki[:], in_=tmp[:]), r=[tmp_tk], w=[ki_tk])
            b.V(lambda e: e.tensor_copy(out=tmp[:], in_=ki[:]), r=[ki_tk], w=[tmp_tk])
            b.V(lambda e: e.tensor_scalar(out=dst[:], in0=ang[:], scalar1=shift, scalar2=None, op0=ALU.add),
                r=[ang_tk], w=[dst_tk])
            b.V(lambda e: e.scalar_tensor_tensor(out=dst[:], in0=tmp[:], scalar=-TWO_PI, in1=dst[:],
                                                 op0=ALU.mult, op1=ALU.add), r=[tmp_tk, dst_tk], w=[dst_tk])
            b.V(lambda e: e.tensor_scalar(out=dst[:], in0=dst[:], scalar1=-math.pi, scalar2=math.pi,
                                          op0=ALU.max, op1=ALU.min), r=[dst_tk], w=[dst_tk])
            b.A(lambda e: e.activation(out=dst[:], in_=dst[:], func=AF.Sin), r=[dst_tk], w=[dst_tk])
        b.V(lambda e: e.tensor_scalar(out=sin_t[:], in0=sin_t[:], scalar1=sgn[:, 0:1], scalar2=None, op0=ALU.mult),
            r=[sin_tk, sgn_tk], w=[sin_tk])
        return cos_t, cos_tk, sin_t, sin_tk

    def stage_attn(self):
        b, nc = self.b, self.nc
        qkT, qkT_tok = self.scratch("qkT", [1280, L], F32)
        v_d, v_tok = self.scratch("v_tm", [L, 256], BF16)
        mixT, mixT_tok = self.scratch("mixT", [2048, L], BF16)
        with ExitStack() as es:
            cos_t, cos_tk, sin_t, sin_tk = self.rope_tables(es)
            gq, gq_tk = self.sb(es, "at_g", [64, 4], F32)
            for ci, nm in ((0, "ev_q_norm"), (2, "ev_k_norm")):
                src = self.inp(nm)[0]
                b.dma("sp", gq[:, ci:ci + 1], src.rearrange("(p o) -> p o", o=1), pw=[gq_tk])
                b.dma("sp", gq[0:32, ci + 1:ci + 2], src[32:64].rearrange("(p o) -> p o", o=1), pw=[gq_tk])
                b.dma("sp", gq[32:64, ci + 1:ci + 2], src[0:32].rearrange("(p o) -> p o", o=1), pw=[gq_tk])
            sx, sx_tk = self.sb(es, "at_sx", [64, 16], F32)
            b.dma("sp", sx[:], self.inp("ev_sinks")[0].partition_broadcast(64), w=[sx_tk])
            b.A(lambda e: e.activation(out=sx[:], in_=sx[:], func=AF.Exp), r=[sx_tk], w=[sx_tk])
            v_sb, v_sb_tk = self.sb(es, "at_v", [128, 16, 256], BF16)
            b.dma("sp", v_sb[:], v_d.rearrange("(n p) c -> p n c", p=128), r=[v_tok], w=[v_sb_tk])
            rawp = Pool_(es, nc, "at_raw", [64, L], F32, 2)
            rswp = Pool_(es, nc, "at_rsw", [64, L], F32, 2)
            qr, qr_tk = self.sb(es, "at_qr", [64, 4, L], BF16)
            kr, kr_tk = self.sb(es, "at_kr", [64, L], BF16)
            sqp = Pool_(es, nc, "at_sq", [64, 512], BF16, 2)
            t1p = Pool_(es, nc, "at_t1", [64, 512], F32, 2)
            t2p = Pool_(es, nc, "at_t2", [64, 512], F32, 2)
            t3p = Pool_(es, nc, "at_t3", [64, 512], F32, 2)
            ptp = Pool_(es, nc, "at_pt", [128, 512], BF16, 4)
            dnp = Pool_(es, nc, "at_dn", [64, 512], F32, 2)
            otp = Pool_(es, nc, "at_ot", [64, 512], BF16, 3)
            ones_b = self.c["c_ones_b"]
            ones_tk = self.ctok["c_ones_b"]
            bi = 0
            for g in range(4):
                heads = [(4 * g + i) * 64 for i in range(4)] + [1024 + g * 64]
                for hi, r0 in enumerate(heads):
                    raw, raw_tk = rawp.next()
                    rsw, rsw_tk = rswp.next()
                    b.dma("sp", raw[:], qkT[r0:r0 + 64, :], r=[qkT_tok], w=[raw_tk])
                    b.dma("sp", rsw[0:32, :], qkT[r0 + 32:r0 + 64, :], r=[qkT_tok], w=[rsw_tk])
                    b.dma("sp", rsw[32:64, :], qkT[r0:r0 + 32, :], r=[qkT_tok], pw=[rsw_tk])
                    gc = 0 if hi < 4 else 2
                    for tt in range(4):
                        sl = slice(tt * 512, (tt + 1) * 512)
                        sq, sq_tk = sqp.next()
                        b.A(lambda e: e.activation(out=sq[:], in_=raw[:, sl], func=AF.Square), r=[raw_tk], w=[sq_tk])
                        bk = bi % 2
                        bi += 1
                        ps, pstk = self.ps[bk], self.pst[bk]
                        b.T(lambda e: e.matmul(ps[0:64, :], lhsT=ones_b[0:64, 0:64], rhs=sq[:], start=True, stop=True),
                            r=[sq_tk, ones_tk], w=[pstk])
                        t1, t1_tk = t1p.next()
                        b.A(lambda e: e.activation(out=t1[:], in_=ps[0:64, :], func=AF.Sqrt, scale=1.0 / 64,
                                                   bias=self.eps_t[0:64, 0:1]), r=[pstk, self.eps_tok], w=[t1_tk])
                        b.V(lambda e: e.reciprocal(out=t1[:], in_=t1[:]), r=[t1_tk], w=[t1_tk])
                        t2, t2_tk = t2p.next()
                        b.V(lambda e: e.scalar_tensor_tensor(out=t2[:], in0=raw[:, sl], scalar=gq[:, gc:gc + 1],
                                                             in1=cos_t[:, sl], op0=ALU.mult, op1=ALU.mult),
                            r=[raw_tk, gq_tk, cos_tk], w=[t2_tk])
                        t3, t3_tk = t3p.next()
                        b.V(lambda e: e.scalar_tensor_tensor(out=t3[:], in0=rsw[:, sl], scalar=gq[:, gc + 1:gc + 2],
                                                             in1=sin_t[:, sl], op0=ALU.mult, op1=ALU.mult),
                            r=[rsw_tk, gq_tk, sin_tk], w=[t3_tk])
                        b.G(lambda e: e.tensor_tensor(out=t2[:], in0=t2[:], in1=t3[:], op=ALU.add),
                            r=[t2_tk, t3_tk], w=[t2_tk])
                        dst = qr[:, hi, sl] if hi < 4 else kr[:, sl]
                        dtk = qr_tk if hi < 4 else kr_tk
                        b.G(lambda e: e.tensor_tensor(out=dst, in0=t2[:], in1=t1[:], op=ALU.mult),
                            r=[t2_tk, t1_tk], pw=[dtk])
                for n in range(16):
                    js = [j for j in (n - 1, n) if j >= 0]
                    pts = []
                    for j in js:
                        bk = 2 + (bi % 2)
                        bi += 1
                        ps, pstk = self.ps[bk], self.pst[bk]
                        b.T(lambda e: e.matmul(ps[:, :].rearrange("p (i t) -> p i t", i=4),
                                               lhsT=kr[:, j * 128:(j + 1) * 128],
                                               rhs=qr[:, :, n * 128:(n + 1) * 128], start=True, stop=True),
                            r=[kr_tk, qr_tk], w=[pstk])
                        pt, pt_tk = ptp.next()
                        b.A(lambda e: e.activation(out=pt[:], in_=ps[:], func=AF.Exp, scale=0.125), r=[pstk], w=[pt_tk])
                        mk = "c_mcur_b" if j == n else "c_mprev_b"
                        mask = self.c[mk][:, :].unsqueeze(1).to_broadcast([128, 4, 128])
                        b.VG(lambda e: e.tensor_tensor(out=pt[:, :].rearrange("p (i t) -> p i t", i=4),
                                                       in0=pt[:, :].rearrange("p (i t) -> p i t", i=4),
                                                       in1=mask, op=ALU.mult), r=[pt_tk, self.ctok[mk]], w=[pt_tk])
                        pts.append((j, pt, pt_tk))
                    pso, psotk = self.ps[4 + n % 2], self.pst[4 + n % 2]
                    psd, psdtk = self.ps[6 + n % 2], self.pst[6 + n % 2]
                    for ii, (j, pt, pt_tk) in enumerate(pts):
                        b.T(lambda e: e.matmul(pso[0:64, :], lhsT=v_sb[:, j, g * 64:(g + 1) * 64], rhs=pt[:],
                                               start=(ii == 0), stop=(ii == len(pts) - 1)),
                            r=[v_sb_tk, pt_tk], w=[psotk] if ii == 0 else [], pw=[psotk] if ii else [],
                            inc=(ii == len(pts) - 1))
                    for ii, (j, pt, pt_tk) in enumerate(pts):
                        b.T(lambda e: e.matmul(psd[0:64, :], lhsT=ones_b[:, 0:64], rhs=pt[:],
                                               start=(ii == 0), stop=(ii == len(pts) - 1)),
                            r=[ones_tk, pt_tk], w=[psdtk] if ii == 0 else [], pw=[psdtk] if ii else [],
                            inc=(ii == len(pts) - 1))
                    dn, dn_tk = dnp.next()
                    sxb = sx[:, 4 * g:4 * g + 4].unsqueeze(2).to_broadcast([64, 4, 128])
                    b.V(lambda e: e.tensor_tensor(out=dn[:, :].rearrange("p (i t) -> p i t", i=4),
                                                  in0=psd[0:64, :].rearrange("p (i t) -> p i t", i=4),
                                                  in1=sxb, op=ALU.add), r=[psdtk, sx_tk], w=[dn_tk])
                    b.V(lambda e: e.reciprocal(out=dn[:], in_=dn[:]), r=[dn_tk], w=[dn_tk])
                    ot, ot_tk = otp.next()
                    b.V(lambda e: e.tensor_tensor(out=ot[:], in0=pso[0:64, :], in1=dn[:], op=ALU.mult),
                        r=[psotk, dn_tk], w=[ot_tk])
                    b.dma("sp", mixT[g * 256:(g + 1) * 256, n * 128:(n + 1) * 128].rearrange("(i d) t -> d i t", d=64),
                          ot[:, :].rearrange("p (i t) -> p i t", i=4), r=[ot_tk], pw=[mixT_tok])
            b.barrier()

    def stage_outproj(self, mixT, mixT_tok, W, wtok, xin, xin_tok, xin_row0, xout, xout_tok, xout_row0, tag):
        b, nc = self.b, self.nc
        with ExitStack() as es:
            mT, mT_tok = self.sb(es, "mT_" + tag, [128, 16, L], BF16)
            for kc in range(16):
                b.dma("sp", mT[:, kc, :], mixT[kc * 128:(kc + 1) * 128, :], r=[mixT_tok], pw=[mT_tok])
            wpool = Pool_(es, nc, "wD" + tag, [128, 16, 512], BF16, 3)
            xp = Pool_(es, nc, "xD" + tag, [128, 512], F32, 4)

            def evac(gi, ncols, tt, ps, pstk):
                xt, xtk = xp.next()
                b.dma("sp", xt[:], xin[xin_row0 + tt * 128: xin_row0 + (tt + 1) * 128, gi * 512:(gi + 1) * 512],
                      r=[xin_tok], w=[xtk])
                b.V(lambda e: e.tensor_tensor(out=xt[:], in0=ps[:], in1=xt[:], op=ALU.add), r=[pstk, xtk], w=[xtk])
                b.dma("sp", xout[xout_row0 + tt * 128: xout_row0 + (tt + 1) * 128, gi * 512:(gi + 1) * 512], xt[:],
                      r=[xtk], pw=[xout_tok])

            self.linear_tm(wpool, W, wtok, [(c, 512, c // 512) for c in range(0, D, 512)], mT, mT_tok, L, evac,
                           [0, 1, 2, 3])
            b.barrier()

    def stage_mlp(self, layer, xin, xin_tok, xin_row0, xout, xout_tok, xout_row0):
        b, nc = self.b, self.nc
        Wu = self.inp("w_up")[layer]
        Wd = self.inp("w_down")[layer]
        wu_tok, wd_tok = self.dtok["w_up"], self.dtok["w_down"]
        tag = "E%d" % layer
        with ExitStack() as es:
            g_sb, g_tok = self.load_vec_fm(es, "g_" + tag, self.inp("norm_ffn")[layer], 16)
            hT, hT_tok = self.sb(es, "hT_" + tag, [128, 16, 512], BF16)
            uT, uT_tok = self.sb(es, "uT_" + tag, [128, 64, 512], BF16)
            xpool = Pool_(es, nc, "x" + tag, [128, D], F32, 2)
            xspool = Pool_(es, nc, "xs" + tag, [128, D], F32, 2)
            stat = Pool_(es, nc, "st" + tag, [128, 4], F32, 2)
            wpool = Pool_(es, nc, "w" + tag, [128, 16, 512], BF16, 3)
            rp = Pool_(es, nc, "r" + tag, [128, 512], BF16, 3)
            xo = Pool_(es, nc, "xo" + tag, [128, 512], F32, 4)
            for t4 in range(L // 512):
                r0 = xin_row0 + t4 * 512
                self.norm_transpose((xpool, xspool, stat), xin, xin_tok, r0, 512, g_sb, g_tok, hT, hT_tok, 0, [6, 7])

                def evac_up(oc, m, tt, ps, pstk):
                    rt, rtk = rp.next()
                    b.A(lambda e: e.activation(out=rt[:], in_=ps[:], func=AF.Relu), r=[pstk], w=[rtk])
                    b.VG(lambda e: e.tensor_tensor(out=uT[:, oc, :], in0=rt[:], in1=rt[:], op=ALU.mult),
                         r=[rtk], pw=[uT_tok])

                self.linear_fm(wpool, Wu, wu_tok, [(c, 512, c // 128) for c in range(0, DFF, 512)], hT, hT_tok, 512,
                               evac_up, [4, 5])
                for cg in range(4):
                    for kq in range(4):
                        wt, wtk = self.load_piece(wpool, Wd, wd_tok, kq * 2048, 16, cg * 512, 512)
                        for s in range(4):
                            ps, pstk = self.ps[s], self.pst[s]
                            for fc in range(16):
                                first = (kq == 0 and fc == 0)
                                last = (kq == 3 and fc == 15)
                                b.T(lambda e: e.matmul(ps[:, :], lhsT=uT[:, kq * 16 + fc, s * 128:(s + 1) * 128],
                                                       rhs=wt[:, fc, :], start=first, stop=last),
                                    r=[wtk, uT_tok], w=[pstk] if first else [], pw=[] if first else [pstk],
                                    inc=(fc == 15))
                    for s in range(4):
                        ps, pstk = self.ps[s], self.pst[s]
                        xt, xtk = xo.next()
                        rr = r0 + s * 128
                        b.dma("sp", xt[:], xin[rr:rr + 128, cg * 512:(cg + 1) * 512], r=[xin_tok], w=[xtk])
                        b.V(lambda e: e.tensor_tensor(out=xt[:], in0=ps[:], in1=xt[:], op=ALU.add), r=[pstk, xtk], w=[xtk])
                        ro = xout_row0 + t4 * 512 + s * 128
                        b.dma("sp", xout[ro:ro + 128, cg * 512:(cg + 1) * 512], xt[:], r=[xtk], pw=[xout_tok])
            b.barrier()

    def stage_ssd(self):
        b, nc = self.b, self.nc
        xbcT, xbcT_tok = self.scratch("xbcT", [1536, L], F32)
        z_d, z_tok = self.scratch("z_tm", [L, 1024], F32)
        dt_d, dt_tok = self.scratch("dtr_tm", [L, 16], F32)
        mixT, mixT_tok = self.scratch("mixT", [2048, L], BF16)
        NCH = L // 128
        with ExitStack() as es:
            cw, cw_tk = self.sb(es, "sd_cw", [128, 12, 4], F32)
            for j in range(4):
                b.dma("sp", cw[:, :, j], self.inp("ev_conv_w")[0][j].rearrange("(c p) -> p c", p=128), pw=[cw_tk],
                      allow_slow_non_contiguous=True)
            cb, cb_tk = self.load_vec_fm(es, "sd_cb", self.inp("ev_conv_b")[0], 12)
            prm, prm_tk = self.sb(es, "sd_prm", [128, 3, 16], F32)
            b.dma("sp", prm[:, 0, :], self.inp("ev_dt_bias")[0].partition_broadcast(128), w=[prm_tk])
            b.dma("sp", prm[:, 1, :], self.inp("ev_a_log")[0].partition_broadcast(128), pw=[prm_tk])
            b.dma("sp", prm[:, 2, :], self.inp("ev_d_skip")[0].partition_broadcast(128), pw=[prm_tk])
            b.A(lambda e: e.activation(out=prm[:, 1, :], in_=prm[:, 1, :], func=AF.Exp), r=[prm_tk], w=[prm_tk])
            b.V(lambda e: e.tensor_scalar(out=prm[:, 1, :], in0=prm[:, 1, :], scalar1=-1.0, scalar2=None, op0=ALU.mult),
                r=[prm_tk], w=[prm_tk])
            nrm, nrm_tk = self.sb(es, "sd_nrm", [128, 1024], F32)
            b.dma("sp", nrm[:], self.inp("ev_ssd_norm")[0].partition_broadcast(128), w=[nrm_tk])
            act, act_tk = self.sb(es, "sd_act", [128, 12, L], BF16)
            rawp = Pool_(es, nc, "sd_raw", [128, L + 3], F32, 2)
            accp = Pool_(es, nc, "sd_acc", [128, L], F32, 2)
            for c in range(12):
                raw, raw_tk = rawp.next()
                b.V(lambda e: e.memset(raw[:, 0:3], 0.0), w=[raw_tk])
                b.dma("sp", raw[:, 3:3 + L], xbcT[c * 128:(c + 1) * 128, :], r=[xbcT_tok], pw=[raw_tk])
                acc, acc_tk = accp.next()
                b.V(lambda e: e.tensor_scalar(out=acc[:], in0=raw[:, 3:3 + L], scalar1=cw[:, c, 3:4], scalar2=cb[:, c:c + 1],
                                              op0=ALU.mult, op1=ALU.add), r=[raw_tk, cw_tk, cb_tk], w=[acc_tk])
                for j in range(3):
                    b.V(lambda e: e.scalar_tensor_tensor(out=acc[:], in0=raw[:, j:j + L], scalar=cw[:, c, j:j + 1],
                                                         in1=acc[:], op0=ALU.mult, op1=ALU.add),
                        r=[raw_tk, cw_tk, acc_tk], w=[acc_tk])
                b.A(lambda e: e.activation(out=act[:, c, :], in_=acc[:], func=AF.Silu), r=[acc_tk], pw=[act_tk])
            dtr, dtr_tk = self.sb(es, "sd_dtr", [128, NCH, 16], F32)
            b.dma("sp", dtr[:], dt_d.rearrange("(n p) h -> p n h", p=128), r=[dt_tok], w=[dtr_tk])
            dts, dts_tk = self.sb(es, "sd_dt", [128, NCH, 16], F32)
            dA, dA_tk = self.sb(es, "sd_dA", [128, NCH, 16], F32)
            bc16 = lambda ap: ap.unsqueeze(1).to_broadcast([128, NCH, 16])
            b.V(lambda e: e.tensor_tensor(out=dts[:], in0=dtr[:], in1=bc16(prm[:, 0, :]), op=ALU.add),
                r=[dtr_tk, prm_tk], w=[dts_tk])
            b.A(lambda e: e.activation(out=dts[:], in_=dts[:], func=AF.Exp), r=[dts_tk], w=[dts_tk])
            b.A(lambda e: e.activation(out=dts[:], in_=dts[:], func=AF.Ln, bias=self.one_t[:, 0:1]), r=[dts_tk, self.one_tok], w=[dts_tk])
            b.V(lambda e: e.tensor_tensor(out=dA[:], in0=dts[:], in1=bc16(prm[:, 1, :]), op=ALU.mult),
                r=[dts_tk, prm_tk], w=[dA_tk])
            cs, cs_tk = self.sb(es, "sd_cs", [128, NCH, 16], F32)
            ncs, ncs_tk = self.sb(es, "sd_ncs", [128, NCH, 16], F32)
            csl, csl_tk = self.sb(es, "sd_csl", [128, NCH, 16], F32)
            ecs, ecs_tk = self.sb(es, "sd_ecs", [128, NCH, 16], F32)
            dout, dout_tk = self.sb(es, "sd_dout", [128, NCH, 16], F32)
            cdec, cdec_tk = self.sb(es, "sd_cdec", [128, NCH, 16], F32)
            ps, pstk = self.ps[0], self.pst[0]
            for n in range(NCH):
                b.T(lambda e: e.matmul(ps[:, n * 16:(n + 1) * 16], lhsT=self.c["c_utri"][:], rhs=dA[:, n, :],
                                       start=True, stop=True), r=[dA_tk, self.ctok["c_utri"]],
                    w=[pstk] if n == 0 else [], pw=[pstk] if n else [], inc=(n == NCH - 1))
            b.V(lambda e: e.tensor_copy(out=cs[:, :, :].rearrange("p n h -> p (n h)"), in_=ps[:, 0:NCH * 16]), r=[pstk], w=[cs_tk])
            ps, pstk = self.ps[1], self.pst[1]
            for n in range(NCH):
                b.T(lambda e: e.matmul(ps[:, n * 16:(n + 1) * 16], lhsT=self.c["c_last"][:], rhs=cs[:, n, :],
                                       start=True, stop=True), r=[cs_tk, self.ctok["c_last"]],
                    w=[pstk] if n == 0 else [], pw=[pstk] if n else [], inc=(n == NCH - 1))
            b.V(lambda e: e.tensor_copy(out=csl[:, :, :].rearrange("p n h -> p (n h)"), in_=ps[:, 0:NCH * 16]), r=[pstk], w=[csl_tk])
            b.V(lambda e: e.tensor_scalar(out=ncs[:], in0=cs[:], scalar1=-1.0, scalar2=None, op0=ALU.mult), r=[cs_tk], w=[ncs_tk])
            b.A(lambda e: e.activation(out=ecs[:], in_=cs[:], func=AF.Exp), r=[cs_tk], w=[ecs_tk])
            b.A(lambda e: e.activation(out=cdec[:], in_=csl[:], func=AF.Exp), r=[csl_tk], w=[cdec_tk])
            b.V(lambda e: e.tensor_tensor(out=dout[:], in0=csl[:], in1=cs[:], op=ALU.subtract), r=[csl_tk, cs_tk], w=[dout_tk])
            b.A(lambda e: e.activation(out=dout[:], in_=dout[:], func=AF.Exp), r=[dout_tk], w=[dout_tk])
            dbc, dbc_tk = self.sb(es, "sd_dbc", [128, 16, 64], F32)
            b.V(lambda e: e.tensor_copy(out=dbc[:], in_=prm[:, 2, :].unsqueeze(2).to_broadcast([128, 16, 64])),
                r=[prm_tk], w=[dbc_tk])
            st, st_tk = self.sb(es, "sd_st", [128, 1024], F32)
            stb, stb_tk = self.sb(es, "sd_stb", [128, 1024], BF16)
            b.V(lambda e: e.memset(st[:], 0.0), w=[st_tk])
            b.V(lambda e: e.memset(stb[:], 0.0), w=[stb_tk])
            xsp = Pool_(es, nc, "sd_xs", [128, 16, 64], BF16, 2)
            xbp = Pool_(es, nc, "sd_xb", [128, 16, 64], BF16, 2)
            xdp = Pool_(es, nc, "sd_xd", [128, 16, 64], BF16, 2)
            btp = Pool_(es, nc, "sd_bt", [128, 256], BF16, 2)
            cbp = Pool_(es, nc, "sd_cbt", [128, 2, 128], F32, 2)
            dAUp = Pool_(es, nc, "sd_dAU", [128, 16, 128], F32, 2)
            ltp = Pool_(es, nc, "sd_lt", [128, 16, 128], F32, 2)
            mtp = Pool_(es, nc, "sd_mt", [128, 16, 128], BF16, 2)
            yop = Pool_(es, nc, "sd_yo", [128, 16, 64], F32, 2)
            zp = Pool_(es, nc, "sd_z", [128, 1024], F32, 2)
            y4p = Pool_(es, nc, "sd_y4", [128, 1024], BF16, 2)
            ytp = Pool_(es, nc, "sd_yt", [128, 8, 128], BF16, 2)
            ssp = Pool_(es, nc, "sd_ss", [128, 4], F32, 2)
            identb = self.c["c_ident_b"]
            identb_tk = self.ctok["c_ident_b"]
            for n in range(NCH):
                tsl = slice(n * 128, (n + 1) * 128)
                bc64 = lambda ap: ap.unsqueeze(2).to_broadcast([128, 16, 64])
                p0, p0tk = self.ps[0], self.pst[0]
                p0b = p0[:, :].bitcast(BF16)
                for c8 in range(8):
                    b.T(lambda e: e.transpose(out=p0b[:, c8 * 128:(c8 + 1) * 128], in_=act[:, c8, tsl], identity=identb[:]),
                        r=[act_tk, identb_tk], w=[p0tk] if c8 == 0 else [], pw=[p0tk] if c8 else [], inc=(c8 == 7))
                p1, p1tk = self.ps[1], self.pst[1]
                p1b = p1[:, :].bitcast(BF16)
                for g in range(2):
                    b.T(lambda e: e.transpose(out=p1b[:, g * 128:(g + 1) * 128], in_=act[:, 8 + g, tsl], identity=identb[:]),
                        r=[act_tk, identb_tk], w=[p1tk] if g == 0 else [], pw=[p1tk] if g else [], inc=False)
                for g in range(2):
                    b.T(lambda e: e.matmul(p1[:, 256 + g * 128:256 + (g + 1) * 128], lhsT=act[:, 8 + g, tsl],
                                           rhs=act[:, 10 + g, tsl], start=True, stop=True),
                        r=[act_tk], pw=[p1tk], inc=(g == 1))
                xs_t, xs_tk = xsp.next()
                b.A(lambda e: e.activation(out=xs_t[:, :, :].rearrange("p h d -> p (h d)"), in_=p0b[:, 0:1024], func=AF.Copy),
                    r=[p0tk], w=[xs_tk])
                xb_t, xb_tk = xbp.next()
                b.V(lambda e: e.tensor_tensor(out=xb_t[:], in0=p0b[:, 0:1024].rearrange("p (h d) -> p h d", h=16),
                                              in1=bc64(dts[:, n, :]), op=ALU.mult), r=[p0tk, dts_tk], w=[xb_tk])
                xd_t, xd_tk = xdp.next()
                b.G(lambda e: e.tensor_tensor(out=xd_t[:], in0=xb_t[:], in1=bc64(dout[:, n, :]), op=ALU.mult),
                    r=[xb_tk, dout_tk], w=[xd_tk])
                bt_t, bt_tk = btp.next()
                b.A(lambda e: e.activation(out=bt_t[:], in_=p1b[:, 0:256], func=AF.Copy), r=[p1tk], w=[bt_tk])
                cb_t, cbt_tk = cbp.next()
                b.V(lambda e: e.tensor_copy(out=cb_t[:, :, :].rearrange("p g q -> p (g q)"), in_=p1[:, 256:512]),
                    r=[p1tk], w=[cbt_tk])
                if n == 1:
                    self.dbg("d_xs", xs_t[:], xs_tk, [128, 16, 64], BF16)
                    self.dbg("d_xb", xb_t[:], xb_tk, [128, 16, 64], BF16)
                    self.dbg("d_cb", cb_t[:], cbt_tk, [128, 2, 128], F32)
                    self.dbg("d_bt", bt_t[:], bt_tk, [128, 256], BF16)
                    self.dbg("d_cs", cs[:], cs_tk, [128, NCH, 16], F32)
                    self.dbg("d_dt", dts[:], dts_tk, [128, NCH, 16], F32)
                dAU, dAU_tk = dAUp.next()
                b.V(lambda e: e.tensor_tensor(out=dAU[:], in0=self.c["c_utri"][:, :].unsqueeze(1).to_broadcast([128, 16, 128]),
                                              in1=dA[:, n, :].unsqueeze(2).to_broadcast([128, 16, 128]), op=ALU.mult),
                    r=[dA_tk, self.ctok["c_utri"]], w=[dAU_tk])
                lt, lt_tk = ltp.next()
                for h in range(16):
                    bk = 2 + h // 4
                    pp, pptk = self.ps[bk], self.pst[bk]
                    col = (h % 4) * 128
                    b.T(lambda e: e.matmul(pp[:, col:col + 128], lhsT=self.c["c_ones"][:], rhs=dAU[:, h, :], start=True, stop=False),
                        r=[dAU_tk, self.ctok["c_ones"]], w=[pptk] if h % 4 == 0 else [], pw=[pptk] if h % 4 else [], inc=False)
                    b.T(lambda e: e.matmul(pp[:, col:col + 128], lhsT=self.c["c_ident"][:], rhs=self.c["c_negm"][:], start=False, stop=True),
                        r=[self.ctok["c_ident"], self.ctok["c_negm"]], pw=[pptk], inc=(h % 4 == 3))
                for h in range(16):
                    bk = 2 + h // 4
                    pp, pptk = self.ps[bk], self.pst[bk]
                    col = (h % 4) * 128
                    b.A(lambda e: e.activation(out=lt[:, h, :], in_=pp[:, col:col + 128], func=AF.Exp, bias=ncs[:, n, h:h + 1]),
                        r=[pptk, ncs_tk], w=[lt_tk] if h == 0 else [], pw=[lt_tk] if h else [])
                mt, mt_tk = mtp.next()
                for g in range(2):
                    b.VG(lambda e: e.tensor_tensor(out=mt[:, g * 8:(g + 1) * 8, :], in0=lt[:, g * 8:(g + 1) * 8, :],
                                                   in1=cb_t[:, g, :].unsqueeze(1).to_broadcast([128, 8, 128]), op=ALU.mult),
                         r=[lt_tk, cbt_tk], w=[mt_tk] if g == 0 else [], pw=[mt_tk] if g else [])
                if n == 1:
                    self.dbg("d_lt", lt[:], lt_tk, [128, 16, 128], F32)
                    self.dbg("d_mt", mt[:], mt_tk, [128, 16, 128], BF16)
                yo, yo_tk = yop.next()
                for g in range(2):
                    pp, pptk = self.ps[6 + g], self.pst[6 + g]
                    b.T(lambda e: e.matmul(pp[:, :], lhsT=act[:, 10 + g, tsl], rhs=stb[:, g * 512:(g + 1) * 512], start=True, stop=True),
                        r=[act_tk, stb_tk], w=[pptk])
                    b.V(lambda e: e.tensor_tensor(out=yo[:, g * 8:(g + 1) * 8, :], in0=pp[:, :].rearrange("p (h d) -> p h d", h=8),
                                                  in1=ecs[:, n, g * 8:(g + 1) * 8].unsqueeze(2).to_broadcast([128, 8, 64]), op=ALU.mult),
                        r=[pptk, ecs_tk], w=[yo_tk] if g == 0 else [], pw=[yo_tk] if g else [])
                for g in range(2):
                    pp, pptk = self.ps[6 + g], self.pst[6 + g]
                    for h8 in range(8):
                        h = g * 8 + h8
                        b.T(lambda e: e.matmul(pp[:, h8 * 64:(h8 + 1) * 64], lhsT=mt[:, h, :], rhs=xb_t[:, h, :], start=True, stop=True),
                            r=[mt_tk, xb_tk], w=[pptk] if h8 == 0 else [], pw=[pptk] if h8 else [], inc=(h8 == 7))
                    b.V(lambda e: e.tensor_tensor(out=yo[:, g * 8:(g + 1) * 8, :], in0=pp[:, :].rearrange("p (h d) -> p h d", h=8),
                                                  in1=yo[:, g * 8:(g + 1) * 8, :], op=ALU.add), r=[pptk, yo_tk], w=[yo_tk])
                if n == 1:
                    self.dbg("d_yo", yo[:], yo_tk, [128, 16, 64], F32)
                zt, zt_tk = zp.next()
                b.dma("sp", zt[:], z_d[n * 128:(n + 1) * 128, :], r=[z_tok], w=[zt_tk])
                b.A(lambda e: e.activation(out=zt[:], in_=zt[:], func=AF.Silu), r=[zt_tk], w=[zt_tk])
                yof = yo[:, :, :].rearrange("p h d -> p (h d)")
                b.G(lambda e: e.tensor_tensor(out=xs_t[:], in0=xs_t[:], in1=dbc[:], op=ALU.mult), r=[xs_tk, dbc_tk], w=[xs_tk])
                b.V(lambda e: e.tensor_tensor(out=yo[:], in0=yo[:], in1=xs_t[:], op=ALU.add), r=[yo_tk, xs_tk], w=[yo_tk])
                b.V(lambda e: e.tensor_tensor(out=yof, in0=yof, in1=zt[:], op=ALU.mult), r=[yo_tk, zt_tk], w=[yo_tk])
                if n == 1:
                    self.dbg("d_y3", yo[:], yo_tk, [128, 16, 64], F32)
                ss, ss_tk = ssp.next()
                for g in range(2):
                    b.V(lambda e: e.scalar_tensor_tensor(out=zt[:, g * 512:(g + 1) * 512], in0=yof[:, g * 512:(g + 1) * 512], scalar=1.0,
                                                         in1=yof[:, g * 512:(g + 1) * 512], op0=ALU.mult, op1=ALU.mult,
                                                         accum_out=ss[:, g:g + 1]), r=[yo_tk], w=[zt_tk, ss_tk] if g == 0 else [zt_tk], pw=[] if g == 0 else [ss_tk])
                b.A(lambda e: e.activation(out=ss[:, 2:4], in_=ss[:, 0:2], func=AF.Sqrt, scale=1.0 / 512, bias=self.eps_t[:, 0:1]),
                    r=[ss_tk, self.eps_tok], w=[ss_tk])
                b.V(lambda e: e.reciprocal(out=ss[:, 2:4], in_=ss[:, 2:4]), r=[ss_tk], w=[ss_tk])
                y4, y4_tk = y4p.next()
                for g in range(2):
                    b.V(lambda e: e.scalar_tensor_tensor(out=y4[:, g * 512:(g + 1) * 512], in0=yof[:, g * 512:(g + 1) * 512],
                                                         scalar=ss[:, 2 + g:3 + g], in1=nrm[:, g * 512:(g + 1) * 512],
                                                         op0=ALU.mult, op1=ALU.mult), r=[yo_tk, ss_tk, nrm_tk],
                        w=[y4_tk] if g == 0 else [], pw=[y4_tk] if g else [])
                for c8 in range(8):
                    b.T(lambda e: e.transpose(out=p0b[:, c8 * 128:(c8 + 1) * 128], in_=y4[:, c8 * 128:(c8 + 1) * 128], identity=identb[:]),
                        r=[y4_tk, identb_tk], w=[p0tk] if c8 == 0 else [], pw=[p0tk] if c8 else [], inc=(c8 == 7))
                yt, yt_tk = ytp.next()
                b.A(lambda e: e.activation(out=yt[:, :, :].rearrange("p c t -> p (c t)"), in_=p0b[:, 0:1024], func=AF.Copy), r=[p0tk], w=[yt_tk])
                b.dma("sp", mixT[1024:2048, tsl].rearrange("(c p) t -> p c t", p=128), yt[:], r=[yt_tk], pw=[mixT_tok])
                for g in range(2):
                    pp, pptk = self.ps[6 + g], self.pst[6 + g]
                    b.T(lambda e: e.matmul(pp[:, :], lhsT=bt_t[:, g * 128:(g + 1) * 128],
                                           rhs=xd_t[:, g * 8:(g + 1) * 8, :].rearrange("p h d -> p (h d)"), start=True, stop=True),
                        r=[bt_tk, xd_tk], w=[pptk])
                    sg = st[:, g * 512:(g + 1) * 512]
                    b.V(lambda e: e.tensor_tensor(out=sg.rearrange("p (h d) -> p h d", h=8), in0=sg.rearrange("p (h d) -> p h d", h=8),
                                                  in1=cdec[:, n, g * 8:(g + 1) * 8].unsqueeze(2).to_broadcast([128, 8, 64]), op=ALU.mult),
                        r=[st_tk, cdec_tk], w=[st_tk])
                    b.V(lambda e: e.tensor_tensor(out=sg, in0=sg, in1=pp[:, :], op=ALU.add), r=[st_tk, pptk], w=[st_tk])
                    b.G(lambda e: e.tensor_copy(out=stb[:, g * 512:(g + 1) * 512], in_=sg), r=[st_tk], w=[stb_tk])
            b.barrier()

    def make_one(self):
        t, tk = self.sb(self.es, "one_t", [128, 1], F32)
        self.b.V(lambda e: e.memset(t[:], 1.0), w=[tk])
        self.one_t, self.one_tok = t, tk

    def stage_rwkv_mix(self, xin, xin_tok, row0):
        b, nc = self.b, self.nc
        xm, xm_tok = self.scratch("xmixT", [6, D, L], BF16)
        with ExitStack() as es:
            g_sb, g_tok = self.load_vec_fm(es, "g_nm1", self.inp("norm_mix")[1], 16)
            mixv, mixv_tk = self.sb(es, "rk_mixv", [128, 6, 16], F32)
            for i in range(6):
                b.dma("sp", mixv[:, i, :], self.inp("od_mix")[0][i].rearrange("(c p) -> p c", p=128), pw=[mixv_tk],
                      allow_slow_non_contiguous=True)
            hT, hT_tok = self.sb(es, "hT_F", [128, 16, L + 1], BF16)
            b.V(lambda e: e.memset(hT[:, :, 0:1], 0.0), pw=[hT_tok])
            xpool = Pool_(es, nc, "xF", [128, D], F32, 2)
            xspool = Pool_(es, nc, "xsF", [128, D], F32, 2)
            stat = Pool_(es, nc, "stF", [128, 4], F32, 2)
            self.norm_transpose((xpool, xspool, stat), xin, xin_tok, row0, L, g_sb, g_tok, hT, hT_tok, 1, [0, 1])
            dp = Pool_(es, nc, "rk_d", [128, L], F32, 2)
            op_ = Pool_(es, nc, "rk_o", [128, L], BF16, 4)
            for kc in range(16):
                d, d_tk = dp.next()
                b.V(lambda e: e.tensor_tensor(out=d[:], in0=hT[:, kc, 0:L], in1=hT[:, kc, 1:L + 1], op=ALU.subtract),
                    r=[hT_tok], w=[d_tk])
                for i in range(6):
                    o, o_tk = op_.next()
                    b.V(lambda e: e.scalar_tensor_tensor(out=o[:], in0=d[:], scalar=mixv[:, i, kc:kc + 1], in1=hT[:, kc, 1:L + 1],
                                                         op0=ALU.mult, op1=ALU.add), r=[d_tk, mixv_tk, hT_tok], w=[o_tk])
                    b.dma("sp", xm[i, kc * 128:(kc + 1) * 128, :], o[:], r=[o_tk], pw=[xm_tok])
            b.barrier()

    def stage_rwkv_proj(self):
        b, nc = self.b, self.nc
        xm, xm_tok = self.scratch("xmixT", [6, D, L], BF16)
        outs = {}
        for nm in ("rT", "kT", "vT", "ewT", "aT", "gT"):
            outs[nm] = self.scratch(nm, [D, L], F32)
        with ExitStack() as es:
            mT, mT_tok = self.sb(es, "mT_G", [128, 16, L], BF16)
            wpool = Pool_(es, nc, "wG", [128, 16, 256], BF16, 3)
            opool = Pool_(es, nc, "oG", [128, 512], F32, 4)
            w0n, w0n_tk = self.load_vec_fm(es, "rk_w0n", self.inp("od_w0")[0], 16)
            b.V(lambda e: e.tensor_scalar(out=w0n[:], in0=w0n[:], scalar1=-1.0, scalar2=None, op0=ALU.mult), r=[w0n_tk], w=[w0n_tk])
            a0, a0_tk = self.load_vec_fm(es, "rk_a0", self.inp("od_a0")[0], 16)
            nhalf, nhalf_tk = self.sb(es, "rk_nhalf", [128, 1], F32)
            b.V(lambda e: e.memset(nhalf[:], -0.5), w=[nhalf_tk])
            lo, lo_tk = self.sb(es, "rk_lo", [128, 2, L], BF16)
            w2s, w2s_tk = self.sb(es, "rk_w2s", [128, 2, D], BF16)

            def load_x(i):
                for kc in range(16):
                    b.dma("sp", mT[:, kc, :], xm[i, kc * 128:(kc + 1) * 128, :], r=[xm_tok], pw=[mT_tok])

            def evac_plain(dst, dst_tok):
                def f(oc, m, tt, ps, pstk):
                    ot, otk = opool.next()
                    b.VA(lambda e: e.tensor_copy(out=ot[:], in_=ps[:]),
                         lambda e: e.activation(out=ot[:], in_=ps[:], func=AF.Copy), r=[pstk], w=[otk])
                    b.dma("sp", dst[oc * 128:(oc + 1) * 128, tt * 512:(tt + 1) * 512], ot[:], r=[otk], pw=[dst_tok])
                return f

            blocks = [(c, 256, c // 128) for c in range(0, D, 256)]
            for i, wn, on in ((0, "od_w_r", "rT"), (2, "od_w_k", "kT"), (3, "od_w_v", "vT")):
                load_x(i)
                self.linear_fm(wpool, self.inp(wn)[0], self.dtok[wn], blocks, mT, mT_tok, L, evac_plain(*outs[on]), [0, 1, 2, 3])

            def lora(i, w1n, r1, func1, w2n, evac2):
                load_x(i)

                def ev1(oc, m, tt, ps, pstk):
                    b.A(lambda e: e.activation(out=lo[0:m, oc, tt * 512:(tt + 1) * 512], in_=ps[0:m, :], func=func1),
                        r=[pstk], pw=[lo_tok_ref[0]])
                lo_tok_ref = [lo_tk]
                self.linear_fm(wpool, self.inp(w1n)[0], self.dtok[w1n], [(c, min(128, r1 - c), c // 128) for c in range(0, r1, 128)],
                               mT, mT_tok, L, ev1, [0, 1, 2, 3])
                nk = (r1 + 127) // 128
                W2 = self.inp(w2n)[0]
                for k in range(nk):
                    rows = min(128, r1 - k * 128)
                    b.dma("pool", w2s[0:rows, k, :], W2[k * 128:k * 128 + rows, :], r=[self.dtok[w2n]], pw=[w2s_tk])
                bi = 0
                for oc in range(16):
                    for tt in range(4):
                        bk = 4 + bi % 4
                        bi += 1
                        ps, pstk = self.ps[bk], self.pst[bk]
                        for k in range(nk):
                            rows = min(128, r1 - k * 128)
                            b.T(lambda e: e.matmul(ps[:, :], lhsT=w2s[0:rows, k, oc * 128:(oc + 1) * 128],
                                                   rhs=lo[0:rows, k, tt * 512:(tt + 1) * 512], start=(k == 0), stop=(k == nk - 1)),
                                r=[w2s_tk, lo_tk], w=[pstk] if k == 0 else [], pw=[pstk] if k else [], inc=(k == nk - 1))
                        evac2(oc, tt, ps, pstk)

            def ev_w(oc, tt, ps, pstk):
                ot, otk = opool.next()
                b.A(lambda e: e.activation(out=ot[:], in_=ps[:], func=AF.Exp, scale=-1.0, bias=w0n[:, oc:oc + 1]), r=[pstk, w0n_tk], w=[otk])
                b.A(lambda e: e.activation(out=ot[:], in_=ot[:], func=AF.Ln, bias=self.one_t[:, 0:1]), r=[otk, self.one_tok], w=[otk])
                b.A(lambda e: e.activation(out=ot[:], in_=ot[:], func=AF.Exp, scale=-1.0, bias=nhalf[:, 0:1]), r=[otk, nhalf_tk], w=[otk])
                b.dma("sp", outs["ewT"][0][oc * 128:(oc + 1) * 128, tt * 512:(tt + 1) * 512], ot[:], r=[otk], pw=[outs["ewT"][1]])

            def ev_a(oc, tt, ps, pstk):
                ot, otk = opool.next()
                b.A(lambda e: e.activation(out=ot[:], in_=ps[:], func=AF.Sigmoid, bias=a0[:, oc:oc + 1]), r=[pstk, a0_tk], w=[otk])
                b.dma("sp", outs["aT"][0][oc * 128:(oc + 1) * 128, tt * 512:(tt + 1) * 512], ot[:], r=[otk], pw=[outs["aT"][1]])

            def ev_g(oc, tt, ps, pstk):
                ot, otk = opool.next()
                b.VA(lambda e: e.tensor_copy(out=ot[:], in_=ps[:]),
                     lambda e: e.activation(out=ot[:], in_=ps[:], func=AF.Copy), r=[pstk], w=[otk])
                b.dma("sp", outs["gT"][0][oc * 128:(oc + 1) * 128, tt * 512:(tt + 1) * 512], ot[:], r=[otk], pw=[outs["gT"][1]])

            lora(1, "od_w1", 96, AF.Tanh, "od_w2", ev_w)
            lora(4, "od_a1", 96, AF.Copy, "od_a2", ev_a)
            lora(5, "od_g1", 256, AF.Sigmoid, "od_g2", ev_g)
            b.barrier()

    def stage_rwkv_prep(self):
        b, nc = self.b, self.nc
        srcs = {nm: self.scratch(nm, [D, L], F32) for nm in ("rT", "kT", "vT", "ewT", "aT")}
        prep, prep_tok = self.scratch("prepT", [7, D, L], F32)
        gC, gC_tok = self.scratch("gC", [D, 16], F32)
        NCH = L // 128
        with ExitStack() as es:
            kk_, kk_tk = self.load_vec_fm(es, "rp_kk", self.inp("od_k_k")[0], 16)
            ka_, ka_tk = self.load_vec_fm(es, "rp_ka", self.inp("od_k_a")[0], 16)
            rk_, rk_tk = self.load_vec_fm(es, "rp_rk", self.inp("od_r_k")[0].rearrange("h d -> (h d)"), 16)
            rmask, rmask_tk = self.sb(es, "rp_rmask", [128, NCH, 128], F32)
            b.V(lambda e: e.memset(rmask[:], 1.0), w=[rmask_tk])
            b.V(lambda e: e.memset(rmask[:, :, 0:1], 0.0), w=[rmask_tk])
            tiny, tiny_tk = self.sb(es, "rp_tiny", [128, 1], F32)
            inp = {nm: Pool_(es, nc, "rp_" + nm, [128, L], F32, 2) for nm in srcs}
            tp = Pool_(es, nc, "rp_t", [128, L], F32, 10)
            gcp = Pool_(es, nc, "rp_gc", [128, 2, NCH], F32, 2)
            blk = self.c["c_blk"]
            blk_tk = self.ctok["c_blk"]
            fl = lambda t: t[:, :]
            for c in range(16):
                rows = slice(c * 128, (c + 1) * 128)
                T_ = {}
                for nm in srcs:
                    t, tk = inp[nm].next()
                    b.dma("sp", t[:], srcs[nm][0][rows, :], r=[srcs[nm][1]], w=[tk])
                    T_[nm] = (t, tk)
                r, r_tk = T_["rT"]
                k, k_tk = T_["kT"]
                v, v_tk = T_["vT"]
                ew, ew_tk = T_["ewT"]
                a, a_tk = T_["aT"]
                kk, kkt = tp.next()
                sq, sqt = tp.next()
                b.V(lambda e: e.tensor_scalar(out=kk[:], in0=k[:], scalar1=kk_[:, c:c + 1], scalar2=None, op0=ALU.mult), r=[k_tk, kk_tk], w=[kkt])
                b.A(lambda e: e.activation(out=sq[:], in_=kk[:], func=AF.Square), r=[kkt], w=[sqt])
                rn, rnt = tp.next()
                for tt in range(4):
                    ps, pstk = self.ps[tt], self.pst[tt]
                    b.T(lambda e: e.matmul(ps[:, :], lhsT=blk[:], rhs=sq[:, tt * 512:(tt + 1) * 512], start=True, stop=True),
                        r=[sqt, blk_tk], w=[pstk])
                    b.V(lambda e: e.tensor_scalar(out=rn[:, tt * 512:(tt + 1) * 512], in0=ps[:, :], scalar1=1e-24, scalar2=None, op0=ALU.max),
                        r=[pstk], w=[rnt] if tt == 0 else [], pw=[rnt] if tt else [])
                b.A(lambda e: e.activation(out=rn[:], in_=rn[:], func=AF.Sqrt), r=[rnt], w=[rnt])
                b.V(lambda e: e.reciprocal(out=rn[:], in_=rn[:]), r=[rnt], w=[rnt])
                b.G(lambda e: e.tensor_tensor(out=kk[:], in0=kk[:], in1=rn[:], op=ALU.mult), r=[kkt, rnt], w=[kkt])
                kf, kft = tp.next()
                b.V(lambda e: e.tensor_scalar(out=kf[:], in0=a[:], scalar1=-1.0, scalar2=ka_[:, c:c + 1], op0=ALU.add, op1=ALU.mult),
                    r=[a_tk, ka_tk], w=[kft])
                b.V(lambda e: e.scalar_tensor_tensor(out=kf[:], in0=kf[:], scalar=1.0, in1=k[:], op0=ALU.add, op1=ALU.mult),
                    r=[kft, k_tk], w=[kft])
                cum, cumt = tp.next()
                b.V(lambda e: e.tensor_tensor_scan(out=cum[:], data0=rmask[:, :, :].rearrange("p n t -> p (n t)"), data1=ew[:],
                                                   initial=0.0, op0=ALU.mult, op1=ALU.add), r=[rmask_tk, ew_tk], w=[cumt])
                gc, gct = gcp.next()
                cumC = cum[:, :].rearrange("p (n t) -> p n t", t=128)[:, :, 127:128]
                b.V(lambda e: e.tensor_copy(out=gc[:, 0, :].unsqueeze(2), in_=cumC), r=[cumt], w=[gct])
                b.A(lambda e: e.activation(out=gc[:, 1, :], in_=gc[:, 0, :], func=AF.Exp, scale=-1.0), r=[gct], w=[gct])
                b.dma("sp", gC[rows, :], gc[:, 1, :], r=[gct], pw=[gC_tok])
                gpos, gpost = tp.next()
                gneg, gnegt = tp.next()
                gprev, gprevt = tp.next()
                gend, gendt = tp.next()
                b.A(lambda e: e.activation(out=gpos[:], in_=cum[:], func=AF.Exp, scale=-1.0), r=[cumt], w=[gpost])
                b.A(lambda e: e.activation(out=gneg[:], in_=cum[:], func=AF.Exp), r=[cumt], w=[gnegt])
                b.G(lambda e: e.tensor_tensor(out=gprev[:], in0=ew[:], in1=cum[:], op=ALU.subtract), r=[ew_tk, cumt], w=[gprevt])
                b.A(lambda e: e.activation(out=gprev[:], in_=gprev[:], func=AF.Exp), r=[gprevt], w=[gprevt])
                b.V(lambda e: e.tensor_tensor(out=gend[:, :].rearrange("p (n t) -> p n t", t=128),
                                              in0=cum[:, :].rearrange("p (n t) -> p n t", t=128),
                                              in1=gc[:, 0, :].unsqueeze(2).to_broadcast([128, NCH, 128]), op=ALU.subtract),
                    r=[cumt, gct], w=[gendt])
                b.A(lambda e: e.activation(out=gend[:], in_=gend[:], func=AF.Exp), r=[gendt], w=[gendt])
                o, ot = tp.next()
                b.V(lambda e: e.scalar_tensor_tensor(out=o[:], in0=kk[:], scalar=-1.0, in1=gprev[:], op0=ALU.mult, op1=ALU.mult),
                    r=[kkt, gprevt], w=[ot])
                b.dma("sp", prep[0, rows, :], o[:], r=[ot], pw=[prep_tok])
                b.G(lambda e: e.tensor_tensor(out=gpos[:], in0=r[:], in1=gpos[:], op=ALU.mult), r=[r_tk, gpost], w=[gpost])
                b.dma("sp", prep[1, rows, :], gpos[:], r=[gpost], pw=[prep_tok])
                b.V(lambda e: e.tensor_tensor(out=kk[:], in0=kk[:], in1=a[:], op=ALU.mult), r=[kkt, a_tk], w=[kkt])
                b.G(lambda e: e.tensor_tensor(out=sq[:], in0=kk[:], in1=gneg[:], op=ALU.mult), r=[kkt, gnegt], w=[sqt])
                b.dma("sp", prep[2, rows, :], sq[:], r=[sqt], pw=[prep_tok])
                b.V(lambda e: e.tensor_tensor(out=gneg[:], in0=kf[:], in1=gneg[:], op=ALU.mult), r=[kft, gnegt], w=[gnegt])
                b.dma("sp", prep[3, rows, :], gneg[:], r=[gnegt], pw=[prep_tok])
                b.G(lambda e: e.tensor_tensor(out=rn[:], in0=kk[:], in1=gend[:], op=ALU.mult), r=[kkt, gendt], w=[rnt])
                b.dma("sp", prep[4, rows, :], rn[:], r=[rnt], pw=[prep_tok])
                b.V(lambda e: e.tensor_tensor(out=gend[:], in0=kf[:], in1=gend[:], op=ALU.mult), r=[kft, gendt], w=[gendt])
                b.dma("sp", prep[5, rows, :], gend[:], r=[gendt], pw=[prep_tok])
                b.V(lambda e: e.scalar_tensor_tensor(out=cum[:], in0=r[:], scalar=rk_[:, c:c + 1], in1=kf[:], op0=ALU.mult, op1=ALU.mult),
                    r=[r_tk, rk_tk, kft], w=[cumt])
                for tt in range(4):
                    ps, pstk = self.ps[4 + tt], self.pst[4 + tt]
                    b.T(lambda e: e.matmul(ps[:, :], lhsT=blk[:], rhs=cum[:, tt * 512:(tt + 1) * 512], start=True, stop=True),
                        r=[cumt, blk_tk], w=[pstk])
                    b.V(lambda e: e.tensor_tensor(out=gprev[:, tt * 512:(tt + 1) * 512], in0=ps[:, :], in1=v[:, tt * 512:(tt + 1) * 512], op=ALU.mult),
                        r=[pstk, v_tk], w=[gprevt] if tt == 0 else [], pw=[gprevt] if tt else [])
                b.dma("sp", prep[6, rows, :], gprev[:], r=[gprevt], pw=[prep_tok])
            b.barrier()

    def stage_rwkv_scan(self):
        b, nc = self.b, self.nc
        prep, prep_tok = self.scratch("prepT", [7, D, L], F32)
        vT, vT_tok = self.scratch("vT", [D, L], F32)
        gC, gC_tok = self.scratch("gC", [D, 16], F32)
        ysc, ysc_tok = self.scratch("y_scan", [L, D], F32)
        NCH = L // 128
        with ExitStack() as es:
            mask4, mask4_tk = self.sb(es, "sc_mask4", [128, 4, 128], F32)
            for i, nm in enumerate(("c_sutri", "c_utri", "c_sutri", "c_utri")):
                b.V(lambda e: e.tensor_copy(out=mask4[:, i, :], in_=self.c[nm][:]), r=[self.ctok[nm]], pw=[mask4_tk])
            gCs, gCs_tk = self.sb(es, "sc_gC", [128, 16, NCH], F32)
            b.dma("sp", gCs[:], gC.rearrange("(c p) n -> p c n", p=128), r=[gC_tok], w=[gCs_tk])
            Hst, _ = self.sb(es, "sc_H", [128, 16, 64], F32)
            Htok = [[Tok("H%d_%d" % (c, hh)) for hh in range(2)] for c in range(16)]
            b.V(lambda e: e.memset(Hst[:], 0.0), w=[t for row in Htok for t in row])
            ARp = Pool_(es, nc, "sc_AR", [128, 2, 128], F32, 4)
            BKp = Pool_(es, nc, "sc_BK", [128, 2, 128], F32, 4)
            TFp = Pool_(es, nc, "sc_TF", [128, 3, 128], F32, 4)
            TKp = Pool_(es, nc, "sc_TK", [128, 3, 128], F32, 4)
            SCp = Pool_(es, nc, "sc_SC", [128, 4, 128], F32, 4)
            XTp = Pool_(es, nc, "sc_XT", [128, 128], F32, 4)
            XXp = Pool_(es, nc, "sc_XX", [128, 2, 128], F32, 6)
            Pp = Pool_(es, nc, "sc_P", [128, 128], F32, 6)
            Gp = Pool_(es, nc, "sc_G", [128, 64], F32, 4)
            Up = Pool_(es, nc, "sc_U", [128, 64], F32, 4)
            Yp = Pool_(es, nc, "sc_Y", [128, 128], F32, 4)
            ident = self.c["c_ident"]
            ident_tk = self.ctok["c_ident"]
            mprev = self.c["c_mprev"]
            mprev_tk = self.ctok["c_mprev"]
            for n in range(NCH):
                tsl = slice(n * 128, (n + 1) * 128)
                for c in range(16):
                    rows = slice(c * 128, (c + 1) * 128)
                    bs = (c % 2) * 4
                    AR, AR_tk = ARp.next()
                    BK, BK_tk = BKp.next()
                    TF, TF_tk = TFp.next()
                    b.dma("sp", AR[:], prep[0:2, rows, tsl].rearrange("i p t -> p i t"), r=[prep_tok], w=[AR_tk])
                    b.dma("sp", BK[:], prep[2:4, rows, tsl].rearrange("i p t -> p i t"), r=[prep_tok], w=[BK_tk])
                    b.dma("sp", TF[:, 0:2, :], prep[4:6, rows, tsl].rearrange("i p t -> p i t"), r=[prep_tok], w=[TF_tk])
                    b.dma("sp", TF[:, 2, :], vT[rows, tsl], r=[vT_tok], pw=[TF_tk])
                    pT, pT_tk = self.ps[bs + 2], self.pst[bs + 2]
                    for i in range(3):
                        b.T(lambda e: e.transpose(out=pT[:, i * 128:(i + 1) * 128], in_=TF[:, i, :], identity=ident[:]),
                            r=[TF_tk, ident_tk], w=[pT_tk] if i == 0 else [], pw=[pT_tk] if i else [], inc=(i == 2))
                    TK, TK_tk = TKp.next()
                    b.A(lambda e: e.activation(out=TK[:, :, :].rearrange("p i t -> p (i t)"), in_=pT[:, 0:384], func=AF.Copy),
                        r=[pT_tk], w=[TK_tk])
                    Y, Y_tk = Yp.next()
                    for hh in range(2):
                        hp = slice(hh * 64, hh * 64 + 64)
                        Hh = Hst[hp, c, :]
                        H_tk = Htok[c][hh]
                        pA, pA_tk = self.ps[bs + 0], self.pst[bs + 0]
                        pB, pB_tk = self.ps[bs + 1], self.pst[bs + 1]
                        pG, pG_tk = self.ps[bs + 3], self.pst[bs + 3]
                        for i in range(2):
                            b.T(lambda e: e.matmul(pA[:, i * 256:(i + 1) * 256].rearrange("p (i t) -> p i t", i=2),
                                                   lhsT=BK[hp, i, :], rhs=AR[hp, :, :], start=True, stop=True),
                                r=[BK_tk, AR_tk], w=[pA_tk] if i == 0 else [], pw=[pA_tk] if i else [], inc=(i == 1))
                        b.T(lambda e: e.matmul(pB[:, 0:128], lhsT=AR[hp, 0, :], rhs=BK[hp, 0, :], start=True, stop=True),
                            r=[BK_tk, AR_tk], w=[pB_tk])
                        SC, SC_tk = SCp.next()
                        b.V(lambda e: e.tensor_tensor(out=SC[:, :, :].rearrange("p i t -> p (i t)"), in0=pA[:, :],
                                                      in1=mask4[:, :, :].rearrange("p i t -> p (i t)"), op=ALU.mult),
                            r=[pA_tk, mask4_tk], w=[SC_tk])
                        XT, XT_tk = XTp.next()
                        b.V(lambda e: e.tensor_tensor(out=XT[:], in0=pB[:, 0:128], in1=mprev[:], op=ALU.mult),
                            r=[pB_tk, mprev_tk], w=[XT_tk])
                        P, P_tk = Pp.next()
                        b.G(lambda e: e.tensor_tensor(out=P[:], in0=SC[:, 0, :], in1=ident[:], op=ALU.add), r=[SC_tk, ident_tk], w=[P_tk])
                        Xc, Xc_tk = SC[:, 0, :], SC_tk
                        XTc, XTc_tk = XT[:], XT_tk
                        for j in range(1, 7):
                            XX, XX_tk = XXp.next()
                            if j < 6:
                                b.T(lambda e: e.matmul(pB[:, 128:256], lhsT=XTc, rhs=Xc, start=True, stop=True),
                                    r=[Xc_tk, XTc_tk], w=[pB_tk], inc=False)
                                b.T(lambda e: e.matmul(pB[:, 256:384], lhsT=Xc, rhs=XTc, start=True, stop=True),
                                    r=[Xc_tk, XTc_tk], pw=[pB_tk])
                                b.A(lambda e: e.activation(out=XX[:, :, :].rearrange("p i t -> p (i t)"), in_=pB[:, 128:384], func=AF.Copy),
                                    r=[pB_tk], w=[XX_tk])
                            else:
                                b.T(lambda e: e.matmul(pB[:, 256:384], lhsT=Xc, rhs=XTc, start=True, stop=True),
                                    r=[Xc_tk, XTc_tk], w=[pB_tk])
                                b.A(lambda e: e.activation(out=XX[:, 1, :], in_=pB[:, 256:384], func=AF.Copy), r=[pB_tk], w=[XX_tk])
                            b.T(lambda e: e.matmul(pT[:, 384:512], lhsT=XX[:, 1, :], rhs=P[:], start=True, stop=True),
                                r=[XX_tk, P_tk], w=[pT_tk])
                            Pn, Pn_tk = Pp.next()
                            b.V(lambda e: e.tensor_tensor(out=Pn[:], in0=pT[:, 384:512], in1=P[:], op=ALU.add), r=[pT_tk, P_tk], w=[Pn_tk])
                            P, P_tk = Pn, Pn_tk
                            Xc, Xc_tk = XX[:, 0, :], XX_tk
                            XTc, XTc_tk = XX[:, 1, :], XX_tk
                        Vh = TK[:, 2, hh * 64:(hh + 1) * 64]
                        b.T(lambda e: e.matmul(pG[:, 0:64], lhsT=AR[hp, 0, :], rhs=Hh, start=True, stop=False),
                            r=[AR_tk, H_tk], w=[pG_tk], inc=False)
                        b.T(lambda e: e.matmul(pG[:, 0:64], lhsT=SC[:, 2, :], rhs=Vh, start=False, stop=True),
                            r=[SC_tk, TK_tk], pw=[pG_tk])
                        Gs, Gs_tk = Gp.next()
                        b.A(lambda e: e.activation(out=Gs[:], in_=pG[:, 0:64], func=AF.Copy), r=[pG_tk], w=[Gs_tk])
                        b.T(lambda e: e.matmul(pG[:, 64:128], lhsT=P[:], rhs=Gs[:], start=True, stop=True),
                            r=[P_tk, Gs_tk], pw=[pG_tk])
                        Us, Us_tk = Up.next()
                        b.V(lambda e: e.tensor_copy(out=Us[:], in_=pG[:, 64:128]), r=[pG_tk], w=[Us_tk])
                        b.T(lambda e: e.matmul(pG[:, 128:192], lhsT=AR[hp, 1, :], rhs=Hh, start=True, stop=False),
                            r=[AR_tk, H_tk], pw=[pG_tk], inc=False)
                        b.T(lambda e: e.matmul(pG[:, 128:192], lhsT=SC[:, 1, :], rhs=Us[:], start=False, stop=False),
                            r=[SC_tk, Us_tk], pw=[pG_tk], inc=False)
                        b.T(lambda e: e.matmul(pG[:, 128:192], lhsT=SC[:, 3, :], rhs=Vh, start=False, stop=True),
                            r=[SC_tk, TK_tk], pw=[pG_tk])
                        b.A(lambda e: e.activation(out=Y[:, hh * 64:(hh + 1) * 64], in_=pG[:, 128:192], func=AF.Copy),
                            r=[pG_tk], w=[Y_tk] if hh == 0 else [], pw=[Y_tk] if hh else [])
                        b.T(lambda e: e.matmul(pG[:, 192:256], lhsT=TK[:, 0, :], rhs=Us[:], start=True, stop=False),
                            r=[TK_tk, Us_tk], pw=[pG_tk], inc=False)
                        b.T(lambda e: e.matmul(pG[:, 192:256], lhsT=TK[:, 1, :], rhs=Vh, start=False, stop=True),
                            r=[TK_tk], pw=[pG_tk])
                        b.V(lambda e: e.scalar_tensor_tensor(out=Hh, in0=Hh, scalar=gCs[hp, c, n:n + 1], in1=pG[hp, 192:256],
                                                             op0=ALU.mult, op1=ALU.add), r=[H_tk, gCs_tk, pG_tk], w=[H_tk])
                    b.dma("sp", ysc[tsl, rows], Y[:], r=[Y_tk], pw=[ysc_tok])
            b.barrier()

    def stage_rwkv_post(self):
        b, nc = self.b, self.nc
        ysc, ysc_tok = self.scratch("y_scan", [L, D], F32)
        prep, prep_tok = self.scratch("prepT", [7, D, L], F32)
        gT, gT_tok = self.scratch("gT", [D, L], F32)
        yoT, yoT_tok = self.scratch("yoT", [D, L], BF16)
        with ExitStack() as es:
            lnw, lnw_tk = self.load_vec_fm(es, "po_lnw", self.inp("od_ln_w")[0], 16)
            lnb, lnb_tk = self.load_vec_fm(es, "po_lnb", self.inp("od_ln_b")[0], 16)
            yp = Pool_(es, nc, "po_y", [128, 32, 64], F32, 2)
            cp = Pool_(es, nc, "po_c", [128, 32, 64], F32, 2)
            sp_ = Pool_(es, nc, "po_s", [128, 32, 64], F32, 2)
            mp = Pool_(es, nc, "po_m", [128, 2, 32], F32, 2)
            bp = Pool_(es, nc, "po_b", [128, 4, 128], F32, 3)
            gp = Pool_(es, nc, "po_g", [128, 4, 128], F32, 3)
            tp = Pool_(es, nc, "po_t", [128, 4, 128], F32, 3)
            op_ = Pool_(es, nc, "po_o", [128, 4, 128], BF16, 3)
            ident = self.c["c_ident"]
            ident_tk = self.ctok["c_ident"]
            bc = lambda ap: ap.unsqueeze(2).to_broadcast([128, 32, 64])
            for n in range(L // 128):
                tsl = slice(n * 128, (n + 1) * 128)
                y, y_tk = yp.next()
                b.dma("sp", y[:, :, :].rearrange("p h d -> p (h d)"), ysc[tsl, :], r=[ysc_tok], w=[y_tk])
                m, m_tk = mp.next()
                b.V(lambda e: e.tensor_reduce(out=m[:, 0, :], in_=y[:], axis=AX.X, op=ALU.add), r=[y_tk], w=[m_tk])
                b.V(lambda e: e.tensor_scalar(out=m[:, 0, :], in0=m[:, 0, :], scalar1=1.0 / 64, scalar2=None, op0=ALU.mult), r=[m_tk], w=[m_tk])
                cen, cen_tk = cp.next()
                b.V(lambda e: e.tensor_tensor(out=cen[:], in0=y[:], in1=bc(m[:, 0, :]), op=ALU.subtract), r=[y_tk, m_tk], w=[cen_tk])
                sq, sq_tk = sp_.next()
                b.G(lambda e: e.tensor_tensor(out=sq[:], in0=cen[:], in1=cen[:], op=ALU.mult), r=[cen_tk], w=[sq_tk])
                b.V(lambda e: e.tensor_reduce(out=m[:, 1, :], in_=sq[:], axis=AX.X, op=ALU.add), r=[sq_tk], w=[m_tk])
                b.A(lambda e: e.activation(out=m[:, 1, :], in_=m[:, 1, :], func=AF.Sqrt, scale=1.0 / 64, bias=self.eps_t[:, 1:2]),
                    r=[m_tk, self.eps_tok], w=[m_tk])
                b.V(lambda e: e.reciprocal(out=m[:, 1, :], in_=m[:, 1, :]), r=[m_tk], w=[m_tk])
                b.V(lambda e: e.tensor_tensor(out=cen[:], in0=cen[:], in1=bc(m[:, 1, :]), op=ALU.mult), r=[cen_tk, m_tk], w=[cen_tk])
                cf = cen[:, :, :].rearrange("p h d -> p (h d)")
                for q4 in range(4):
                    ps, pstk = self.ps[q4], self.pst[q4]
                    for j in range(4):
                        kc = q4 * 4 + j
                        b.T(lambda e: e.transpose(out=ps[:, j * 128:(j + 1) * 128], in_=cf[:, kc * 128:(kc + 1) * 128], identity=ident[:]),
                            r=[cen_tk, ident_tk], w=[pstk] if j == 0 else [], pw=[pstk] if j else [], inc=(j == 3))
                    bt, bt_tk = bp.next()
                    gt, gt_tk = gp.next()
                    rws = slice(q4 * 512, (q4 + 1) * 512)
                    b.dma("sp", bt[:], prep[6, rws, tsl].rearrange("(j p) t -> p j t", p=128), r=[prep_tok], w=[bt_tk])
                    b.dma("sp", gt[:], gT[rws, tsl].rearrange("(j p) t -> p j t", p=128), r=[gT_tok], w=[gt_tk])
                    t1, t1_tk = tp.next()
                    b4 = lambda ap: ap[:, q4 * 4:(q4 + 1) * 4].unsqueeze(2).to_broadcast([128, 4, 128])
                    b.V(lambda e: e.tensor_tensor(out=t1[:], in0=ps[:, :].rearrange("p (j t) -> p j t", j=4), in1=b4(lnw), op=ALU.mult),
                        r=[pstk, lnw_tk], w=[t1_tk])
                    b.G(lambda e: e.tensor_tensor(out=t1[:], in0=t1[:], in1=b4(lnb), op=ALU.add), r=[t1_tk, lnb_tk], w=[t1_tk])
                    b.V(lambda e: e.tensor_tensor(out=t1[:], in0=t1[:], in1=bt[:], op=ALU.add), r=[t1_tk, bt_tk], w=[t1_tk])
                    o, o_tk = op_.next()
                    b.G(lambda e: e.tensor_tensor(out=o[:], in0=t1[:], in1=gt[:], op=ALU.mult), r=[t1_tk, gt_tk], w=[o_tk])
                    b.dma("sp", yoT[rws, tsl].rearrange("(j p) t -> p j t", p=128), o[:], r=[o_tk], pw=[yoT_tok])
            b.barrier()

    def forward(self, x_ap, x_tok, out_ap, out_tok, row0):
        x1, x1_tok = self.scratch("x1", [L, D], F32)
        x2, x2_tok = self.scratch("x2", [L, D], F32)
        x3, x3_tok = self.scratch("x3", [L, D], F32)
        mixT, mixT_tok = self.scratch("mixT", [2048, L], BF16)
        yoT, yoT_tok = self.scratch("yoT", [D, L], BF16)
        self.stage_inproj(x_ap, x_tok, row0)
        self.stage_attn()
        self.stage_ssd()
        self.stage_outproj(mixT, mixT_tok, self.inp("ev_w_out")[0], self.dtok["ev_w_out"], x_ap, x_tok, row0, x1, x1_tok, 0, "D")
        self.stage_mlp(0, x1, x1_tok, 0, x2, x2_tok, 0)
        self.stage_rwkv_mix(x2, x2_tok, 0)
        self.stage_rwkv_proj()
        self.stage_rwkv_prep()
        self.stage_rwkv_scan()
        self.stage_rwkv_post()
        self.stage_outproj(yoT, yoT_tok, self.inp("od_w_o")[0], self.dtok["od_w_o"], x2, x2_tok, 0, x3, x3_tok, 0, "K")
        self.stage_mlp(1, x3, x3_tok, 0, out_ap, out_tok, row0)


_PROG_CACHE = {}


def _build(nb):
    if nb in _PROG_CACHE:
        return _PROG_CACHE[nb]
    p = Prog(nb=nb)
    p.make_eps()
    p.make_one()
    x_ap = p.inp("x", [nb * L, D])
    out_ap = p.nc.dram_tensor("out", [nb * L, D], F32, kind="ExternalOutput").ap()
    out_tok = Tok("out")
    for bi in range(nb):
        p.forward(x_ap, p.dtok["x"], out_ap, out_tok, bi * L)
    p.b.finish()
    _PROG_CACHE[nb] = p
    return p


def kernel(**inputs):
    nb = 8 // NCORES
    p = _build(nb)
    consts = host_consts()
    x = np.ascontiguousarray(np.asarray(inputs["x"], dtype=np.float32)).reshape(8 * L, D)
    in_maps = []
    for c in range(NCORES):
        m = {}
        for name in p.inputs_used:
            if name in consts:
                m[name] = consts[name]
            elif name == "x":
                m[name] = x[c * nb * L:(c + 1) * nb * L]
            else:
                m[name] = np.ascontiguousarray(np.asarray(inputs[name], dtype=np.float32))
        in_maps.append(m)
    res = run_bass_kernel_spmd(p.nc, in_maps, core_ids=list(range(NCORES)))
    out = np.concatenate([np.asarray(r["out"]) for r in res.results], axis=0)
    return out.reshape(8, L, D).astype(np.float32)
```
